# Optimizing a Trainium2 kernel written in Bass

```python
import math
import jax, jax.numpy as jnp
from jax import lax
import numpy as np

D_MODEL = 1024
BATCH = 16
SEQ = 2048
DEPTH = 1

E_A = D_MODEL
CONV_K = 31
N_GROUPS_A = 8
E_B = D_MODEL
CHUNK = 128
N_HEADS_B = 8
HEAD_DIM_B = E_B // N_HEADS_B
N_BRANCH = 2
COLS_A_IN = 2 * E_A
COLS_A_GATE = E_A
COLS_B_UV = 2 * E_B
COLS_B_GATE = E_B
COLS_MERGE = N_BRANCH * D_MODEL
D_IN = COLS_A_IN + COLS_A_GATE + COLS_B_UV + COLS_B_GATE + COLS_MERGE
DEEPNORM_ALPHA = (2.0 * DEPTH) ** 0.25
DEEPNORM_BETA = (8.0 * DEPTH) ** -0.25
LN_EPS = 1e-5

kernel_name = "hybrid_conformer_conv_chunked_gmlp_gated_deepnorm"


def _layer_norm(x, g, b):
    xf = x.astype(jnp.float32)
    mu = jnp.mean(xf, axis=-1, keepdims=True)
    var = jnp.mean(jnp.square(xf - mu), axis=-1, keepdims=True)
    y = (xf - mu) * lax.rsqrt(var + LN_EPS)
    return (y * g.astype(jnp.float32) + b.astype(jnp.float32)).astype(x.dtype)


def _group_norm_channels(x, g, b, n_groups):
    shp = x.shape
    xf = x.astype(jnp.float32).reshape(shp[:-1] + (n_groups, shp[-1] // n_groups))
    mu = jnp.mean(xf, axis=-1, keepdims=True)
    var = jnp.mean(jnp.square(xf - mu), axis=-1, keepdims=True)
    y = ((xf - mu) * lax.rsqrt(var + LN_EPS)).reshape(shp)
    return (y * g.astype(jnp.float32) + b.astype(jnp.float32)).astype(x.dtype)


def _conformer_conv_branch(a_in, a_gate, conv_w, conv_b, gn_g, gn_b, w_pa):
    val, gate = jnp.split(a_in, 2, axis=-1)
    h = val * jax.nn.sigmoid(gate)
    h = lax.conv_general_dilated(
        h, conv_w[:, None, :].astype(h.dtype),
        window_strides=(1,), padding=[(CONV_K - 1, 0)],
        dimension_numbers=("NWC", "WIO", "NWC"),
        feature_group_count=E_A) + conv_b
    h = _group_norm_channels(h, gn_g, gn_b, N_GROUPS_A)
    h = jax.nn.silu(h) * jax.nn.silu(a_gate)
    return jnp.einsum("bse,ed->bsd", h, w_pa)


def _chunked_gmlp_branch(b_uv, b_gate, ln_v_g, ln_v_b, w_spatial, b_spatial, w_pb):
    bsz, seq = b_uv.shape[0], b_uv.shape[1]
    z = jax.nn.gelu(b_uv)
    u, v = jnp.split(z, 2, axis=-1)
    v = _layer_norm(v, ln_v_g, ln_v_b)
    v = v.reshape(bsz, seq // CHUNK, CHUNK, N_HEADS_B, HEAD_DIM_B)
    causal = jnp.tril(jnp.ones((CHUNK, CHUNK), dtype=bool))
    ws = jnp.where(causal[None], w_spatial, jnp.zeros((), w_spatial.dtype))
    v_mix = jnp.einsum("hts,bnshd->bnthd", ws, v) + b_spatial.T[None, None, :, :, None]
    s = u * v_mix.reshape(bsz, seq, E_B)
    s = s * jax.nn.silu(b_gate)
    return jnp.einsum("bse,ed->bsd", s, w_pb)


def setup_inputs(seed: int = 0) -> dict:
    key = jax.random.key(seed)
    ks = jax.random.split(key, 20)
    f32 = jnp.float32
    nrm = lambda k, shp, sc: jax.random.normal(k, shp, f32) * sc
    return {
        "x": jax.random.normal(ks[0], (BATCH, SEQ, D_MODEL), f32),
        "w_in": nrm(ks[1], (D_MODEL, D_IN), D_MODEL ** -0.5),
        "b_in": nrm(ks[2], (D_IN,), 0.02),
        "conv_w": nrm(ks[3], (CONV_K, E_A), CONV_K ** -0.5),
        "conv_b": nrm(ks[4], (E_A,), 0.02),
        "gn_g": 1.0 + nrm(ks[5], (E_A,), 0.02),
        "gn_b": nrm(ks[6], (E_A,), 0.02),
        "ln_v_g": 1.0 + nrm(ks[7], (E_B,), 0.02),
        "ln_v_b": nrm(ks[8], (E_B,), 0.02),
        "w_spatial": nrm(ks[9], (N_HEADS_B, CHUNK, CHUNK), CHUNK ** -0.5),
        "b_spatial": 1.0 + nrm(ks[10], (N_HEADS_B, CHUNK), 0.1),
        "w_pa": nrm(ks[11], (E_A, D_MODEL), E_A ** -0.5),
        "w_pb": nrm(ks[12], (E_B, D_MODEL), E_B ** -0.5),
        "w_o": nrm(ks[13], (D_MODEL, D_MODEL), DEEPNORM_BETA * D_MODEL ** -0.5),
        "b_o": nrm(ks[14], (D_MODEL,), 0.02),
        "ln_out_g": 1.0 + nrm(ks[15], (D_MODEL,), 0.02),
        "ln_out_b": nrm(ks[16], (D_MODEL,), 0.02),
    }


def reference(x, w_in, b_in, conv_w, conv_b, gn_g, gn_b, ln_v_g, ln_v_b,
              w_spatial, b_spatial, w_pa, w_pb, w_o, b_o, ln_out_g, ln_out_b):
    splits = np.cumsum([COLS_A_IN, COLS_A_GATE, COLS_B_UV, COLS_B_GATE]).tolist()
    for _ in range(DEPTH):
        proj = jnp.einsum("bsd,dk->bsk", x, w_in) + b_in
        a_in, a_gate, b_uv, b_gate, merge = jnp.split(proj, splits, axis=-1)
        y_a = _conformer_conv_branch(a_in, a_gate, conv_w, conv_b, gn_g, gn_b, w_pa)
        y_b = _chunked_gmlp_branch(b_uv, b_gate, ln_v_g, ln_v_b, w_spatial, b_spatial, w_pb)
        g_a, g_b = jnp.split(jax.nn.sigmoid(merge), 2, axis=-1)
        mixed = g_a * y_a + g_b * y_b
        sub = jnp.einsum("bsd,de->bse", mixed, w_o) + b_o
        x = _layer_norm(DEEPNORM_ALPHA * x + sub, ln_out_g, ln_out_b)
    return x
```

```python
import numpy as np
from contextlib import ExitStack
import concourse.bass as bass
import concourse.mybir as mybir
from concourse.bass_utils import run_bass_kernel_spmd

F32 = mybir.dt.float32
BF16 = mybir.dt.bfloat16
AF = mybir.ActivationFunctionType
ALU = mybir.AluOpType

D = 1024
T = 1024
NSB = 4
NTOK = T * NSB
KC = 8
CONV_K = 31
PAD = CONV_K - 1
EPS = 1e-5
ALPHA = 2.0 ** 0.25
ND = 16
NPE = CONV_K - ND

PC_BIN = 0
PC_CB = 64
PC_GNG = 72
PC_GNB = 80
PC_LVG = 88
PC_LVB = 96
PC_CW = 104
NPC = PC_CW + 8 * CONV_K

CH_VAL, CH_GATE, CH_AG, CH_U, CH_V, CH_BG, CH_MA, CH_MB = 0, 8, 16, 24, 32, 40, 48, 56


class Tracker:
    def __init__(self):
        self.last_writer = {}
        self.readers = {}
        self.streams = {k: [] for k in ("pe", "act", "dve", "pool", "sp")}
        self.waited = {k: {} for k in self.streams}
        self.count = {}
        self.sems = {}

    def _deps(self, eng, reads, writes, force_sync=False):
        evs = []
        for b in reads:
            w = self.last_writer.get(b)
            if w is not None:
                evs.append((w, "raw", b))
        for b in writes:
            w = self.last_writer.get(b)
            if w is not None:
                evs.append((w, "waw", b))
            for r in self.readers.get(b, ()):
                evs.append((r, "war", b))
        need = {}
        for (sem, val, prod), kind, buf in evs:
            if prod == eng:
                if eng in ("pe",):
                    continue
                if kind == "war":
                    continue
                if not force_sync and not (isinstance(buf, tuple) and buf[0] == "stat"):
                    continue
            if self.waited[eng].get(sem, 0) >= val:
                continue
            if need.get(sem, 0) < val:
                need[sem] = val
        for sem, val in need.items():
            self.waited[eng][sem] = val
        return list(need.items())

    def _commit(self, reads, writes, ev):
        for b in reads:
            self.readers.setdefault(b, []).append(ev)
        for b in writes:
            self.last_writer[b] = ev
            self.readers[b] = []

    def op(self, eng, reads, writes, fns, sem=None, inc=1, force_sync=False):
        if not isinstance(fns, (list, tuple)):
            fns = [fns]
        waits = self._deps(eng, reads, writes, force_sync)
        semname = sem or eng
        self.count[semname] = self.count.get(semname, 0) + inc
        val = self.count[semname]
        ev = (semname, val, eng if sem is None else "dma")
        self._commit(reads, writes, ev)
        self.streams[eng].append((waits, fns, semname, inc))
        return ev

    def replay(self, eng, e):
        for waits, fns, semname, inc in self.streams[eng]:
            for s, v in waits:
                e.wait_ge(self.sems[s], v)
            ins = None
            for f in fns:
                ins = f(e)
            ins.then_inc(self.sems[semname], inc)


def build_nc():
    nc = bass.Bass("TRN2", target_bir_lowering=False)
    dt = nc.dram_tensor
    xT_d = dt("xT", [NSB, 128, KC * T], F32, kind="ExternalInput").ap()
    xtok_d = dt("xtok", [NTOK, D], F32, kind="ExternalInput").ap()
    wA_d = dt("wA", [8, 128, KC * 2 * 128], F32, kind="ExternalInput").ap()
    wG_d = dt("wG", [8, 128, KC * 1 * 128], F32, kind="ExternalInput").ap()
    wB_d = dt("wB", [8, 128, KC * 2 * 128], F32, kind="ExternalInput").ap()
    wM_d = dt("wM", [8, 128, KC * 4 * 128], F32, kind="ExternalInput").ap()
    wV_d = dt("wV", [128, KC * 1024], F32, kind="ExternalInput").ap()
    wO_d = dt("wO", [128, KC * 1024], F32, kind="ExternalInput").ap()
    pc_d = dt("pc", [128, NPC], F32, kind="ExternalInput").ap()
    mats_d = dt("mats", [128, 1280], F32, kind="ExternalInput").ap()
    rows_d = dt("rows", [2, 1024], F32, kind="ExternalInput").ap()
    bc_d = dt("bc", [128, 3072], F32, kind="ExternalInput").ap()
    out_d = dt("out", [NTOK, D], F32, kind="ExternalOutput").ap()
    mscr_d = dt("mscr", [8, 128, NPE * 128], BF16, kind="Internal").ap()

    tk = Tracker()
    with ExitStack() as es:
        def sb(name, shape, dtype):
            return es.enter_context(nc.sbuf_tensor(name, shape, dtype))

        xTb = [sb(f"xT_sb{i}", [128, KC, T], BF16) for i in range(2)]
        cur = {}
        hA = sb("hA", [128, 8, T], BF16)
        big2 = sb("big2", [128, 8, T], BF16)
        sB = sb("sB", [128, 8, T], BF16)
        NRING = 3
        wring = [sb(f"wring{i}", [128, KC, 4, 128], BF16) for i in range(NRING)]
        wres = sb("wres", [128, KC, 1024], BF16)
        Mbuf = [sb(f"Mbuf{i}", [128, NPE, 128], BF16) for i in range(2)]
        negones_bf = sb("negones_bf", [128, 128], BF16)
        hbuf = [sb(f"hbuf{i}", [128, PAD + T], BF16) for i in range(2)]
        tail = sb("tail", [128, 8, PAD], BF16)
        NF = 8
        fs = [sb(f"fs{i}", [128, T], F32) for i in range(NF)]
        NB = 3
        accp = [sb(f"accp{i}", [128, T], F32) for i in range(2)]
        bs = [sb(f"bs{i}", [128, T], BF16) for i in range(NB)]
        pc = sb("pc_sb", [128, NPC], F32)
        cmt = sb("cm_sb", [128, 128], F32)
        brow = sb("brow", [2, 2048], BF16)
        blo = sb("blo", [1, 2048], BF16)
        bc = sb("bc_sb", [128, 2048], F32)
        Cm = sb("Cm", [128, 8, 128], F32)
        wsbf = sb("wsbf", [128, 8, 128], BF16)
        cbias = sb("cbias", [128, 8], F32)
        ones_bf = sb("ones_bf", [128, 128], BF16)
        onesf = sb("onesf", [128, 128], F32)
        ones_row = sb("ones_row", [2, 128], BF16)
        stat = sb("stat", [128, 8, 8, 4], F32)
        ps = [es.enter_context(nc.psum_tensor(f"ps{i}", [128, T], F32)) for i in range(4)]

        semnames = ["brs", "pe", "act", "dve", "pool", "xT0", "xT1", "wres", "mb0", "mb1", "ms0", "ms1", "c0", "c1", "c2", "c3", "c4", "c5", "c6", "rw0", "rw1"] + [f"xl{i}" for i in range(8)] + [f"so{i}" for i in range(8)] + [f"wr{i}" for i in range(NRING)]
        for s in semnames:
            tk.sems[s] = es.enter_context(nc.semaphore(s))
        block = es.enter_context(nc.Block())

        cnt = {"ps": 0, "fs": 0, "bs": 0, "st": 0, "ring": 0, "acc": 0}

        def new_ps():
            i = cnt["ps"] % 4
            cnt["ps"] += 1
            return ps[i], ("ps", i)

        def new_fs():
            i = cnt["fs"] % NF
            cnt["fs"] += 1
            return fs[i], ("fs", i)

        def new_bs():
            i = cnt["bs"] % NB
            cnt["bs"] += 1
            return bs[i], ("bs", i)

        def new_stat():
            i = cnt["st"] % 8
            cnt["st"] += 1
            return stat[:, i], ("stat", i)

        FS = lambda i: ("fs", i)
        XR = lambda i: ("fs", 6 + i)
        xr = [fs[6], fs[7]]
        wsT_t = fs[3]
        um_t = fs[4]
        wsm_t = fs[5]
        bsp_t = fs[2]
        tk.op("sp", [], ["pc"], lambda e: e.dma_start(out=pc[:], in_=pc_d), sem="c0", inc=16)
        tk.op("sp", [], ["cm"], lambda e: e.dma_start(out=cmt[:], in_=mats_d[:, 0:128]), sem="c1", inc=16)
        tk.op("sp", [], [FS(4)], lambda e: e.dma_start(out=um_t[:, 0:128], in_=mats_d[:, 128:256]), sem="c2", inc=16)
        tk.op("sp", [], [FS(3)], lambda e: e.dma_start(out=wsT_t[:], in_=mats_d[:, 256:1280]), sem="c3", inc=16)
        tk.op("sp", [], [XR(0)], lambda e: e.dma_start(out=xr[0][0:1, :], in_=rows_d[0:1, :]), sem="rw0", inc=16)
        tk.op("sp", [], [XR(1)], lambda e: e.dma_start(out=xr[1][0:1, :], in_=rows_d[1:2, :]), sem="rw1", inc=16)
        tk.op("sp", [], [FS(2)], lambda e: e.dma_start(out=bsp_t[:], in_=bc_d[:, 0:1024]), sem="c5", inc=16)
        tk.op("sp", [], ["bc"], lambda e: e.dma_start(out=bc[:], in_=bc_d[:, 1024:3072]), sem="c6", inc=16)
        tk.op("dve", [], ["ones_bf"], lambda e: e.memset(ones_bf[:], 1.0 / 128))
        tk.op("dve", [], ["onesf"], lambda e: e.memset(onesf[:], 1.0))
        tk.op("dve", [], ["negones_bf"], lambda e: e.memset(negones_bf[:], -1.0 / 128))
        tk.op("dve", [], ["ones_row"], lambda e: e.memset(ones_row[:], 1.0))
        tk.op("dve", [], [("tail", c) for c in range(8)], lambda e: e.memset(tail[:], 0.0))
        cm_ap = cmt[:]
        p0, p0id = new_ps()
        tk.op("pe", ["pc", "onesf"], [p0id], lambda e: e.matmul(p0[:, 0:8], onesf[:], pc[:, PC_CB:PC_CB + 8], start=True, stop=True))
        tk.op("dve", [p0id, "pc"], ["cbias"], lambda e: e.scalar_tensor_tensor(
            out=cbias[:], in0=p0[:, 0:8], scalar=-1.0 / 128, in1=pc[:, PC_CB:PC_CB + 8], op0=ALU.mult, op1=ALU.add))
        tk.op("dve", [FS(3), FS(4)], [FS(5)], lambda e: e.tensor_tensor(
            out=wsm_t[:].rearrange("p (h t) -> p h t", h=8), in0=wsT_t[:].rearrange("p (h t) -> p h t", h=8),
            in1=um_t[:, 0:128].unsqueeze(1).broadcast_to([128, 8, 128]), op=ALU.mult))
        tk.op("dve", [FS(5)], ["wsbf"], lambda e: e.tensor_copy(out=wsbf[:].rearrange("p h t -> p (h t)"), in_=wsm_t[:]))
        p1, p1id = new_ps()
        tk.op("pe", [FS(5), "onesf"], [p1id], [
            (lambda e, j=j: e.matmul(p1[:, j * 512:(j + 1) * 512], onesf[:], wsm_t[:, j * 512:(j + 1) * 512], start=True, stop=True))
            for j in range(2)])
        for h in range(8):
            tk.op("dve", [p1id, "pc", FS(2)], [("Cm", h)], lambda e, h=h: e.scalar_tensor_tensor(
                out=Cm[:, h, :], in0=p1[:, h * 128:(h + 1) * 128], scalar=pc[:, PC_LVB + h:PC_LVB + h + 1],
                in1=bsp_t[:, h * 128:(h + 1) * 128], op0=ALU.mult, op1=ALU.add))
        for r in range(2):
            src = xr[r][0:1, :]
            tmp = fs[r][0:1, :]
            tk.op("dve", [XR(r)], ["brow_hi%d" % r], lambda e, r=r, src=src: e.tensor_copy(out=brow[0:1, r * 1024:(r + 1) * 1024], in_=src))
            tk.op("dve", ["brow_hi%d" % r], [FS(r)], lambda e, r=r, tmp=tmp: e.tensor_copy(out=tmp, in_=brow[0:1, r * 1024:(r + 1) * 1024]))
            tk.op("dve", [FS(r), XR(r)], [FS(r)], lambda e, src=src, tmp=tmp: e.tensor_tensor(out=tmp, in0=src, in1=tmp, op=ALU.subtract))
            tk.op("dve", [FS(r)], ["blo%d" % r], lambda e, r=r, tmp=tmp: e.tensor_copy(out=blo[:, r * 1024:(r + 1) * 1024], in_=tmp))
        tk.op("sp", ["blo0", "blo1"], ["brow_lo"], lambda e: e.dma_start(out=brow[1:2, :], in_=blo[:]), sem="brs", inc=16)
        BROW = ["brow_hi0", "brow_hi1", "brow_lo"]
        for c in range(8):
            mb_ = Mbuf[c % 2]
            tk.op("dve", ["cm", "pc"], [("Mbuf", c % 2)], lambda e, c=c, mb_=mb_: e.tensor_tensor(
                out=mb_[:], in0=cm_ap.unsqueeze(1).broadcast_to([128, NPE, 128]),
                in1=pc[:, PC_CW + c * CONV_K + ND:PC_CW + (c + 1) * CONV_K].unsqueeze(2).broadcast_to([128, NPE, 128]), op=ALU.mult))
            tk.op("sp", [("Mbuf", c % 2)], [("mscr", c)], lambda e, c=c, mb_=mb_: e.dma_start(
                out=mscr_d[c], in_=mb_[:].rearrange("p k c -> p (k c)")), sem=f"ms{c % 2}", inc=16)

        units = []
        for sbi in range(NSB):
            units.append(("A", sbi, 0))
            units.append(("A", sbi, 1))
            for c in range(2, 8):
                units.append(("A", sbi, c))
                units.append(("G", sbi, c - 2))
            units.append(("G", sbi, 6))
            units.append(("G", sbi, 7))
            for h in range(8):
                units.append(("H", sbi, h))
            for j in range(8):
                units.append(("M", sbi, j))
        unit_slot = {}
        loaded = set()

        def load_unit(n):
            if n >= len(units) or n in loaded:
                return
            loaded.add(n)
            kind, sbi, i = units[n]
            slot = cnt["ring"] % NRING
            cnt["ring"] += 1
            unit_slot[(kind, sbi, i)] = slot
            ns = {"A": 2, "G": 1, "H": 2, "M": 4}[kind]
            src = {"A": wA_d, "G": wG_d, "H": wB_d, "M": wM_d}[kind][i]
            dst = wring[slot][:, :, 0:ns, :]
            tk.op("pool", [], [("wring", slot)], lambda e: e.dma_start(
                out=dst, in_=src.rearrange("p (k s c) -> p k s c", k=KC, s=ns)), sem=f"wr{slot}", inc=16)

        unit_index = {u: n for n, u in enumerate(units)}

        def proj_job(wslot, s, wid, psid_t, extra_reads=()):
            pst, psid = psid_t
            xT = cur["xT"]
            fns = []
            for k in range(KC):
                for blk in range(2):
                    fns.append(lambda e, k=k, blk=blk: e.matmul(
                        pst[:, blk * 512:(blk + 1) * 512], wring[wslot][:, k, s, :], xT[:, k, blk * 512:(blk + 1) * 512],
                        start=(k == 0), stop=(k == KC - 1)))
            tk.op("pe", [wid, cur["xTid"]] + list(extra_reads), [psid], fns)

        def contract_job(wslot, s, wid, src, srcids, psid_t):
            pst, psid = psid_t
            fns = []
            for k in range(KC):
                for blk in range(2):
                    fns.append(lambda e, k=k, blk=blk: e.matmul(
                        pst[:, blk * 512:(blk + 1) * 512], wring[wslot][:, k, s, :], src[:, k, blk * 512:(blk + 1) * 512],
                        start=(k == 0), stop=(k == KC - 1)))
            tk.op("pe", [wid] + list(srcids), [psid], fns)

        def small_rstd(st, stid, n, n_inv):
            tk.op("dve", [stid], [stid], lambda e: e.tensor_scalar(out=st[:, 2:4, 0:n], in0=st[:, 0:2, 0:n], scalar1=n_inv, scalar2=None, op0=ALU.mult))
            tk.op("dve", [stid], [stid], lambda e: e.tensor_tensor(out=st[:, 4, 0:n], in0=st[:, 2, 0:n], in1=st[:, 2, 0:n], op=ALU.mult))
            tk.op("dve", [stid], [stid], lambda e: e.tensor_tensor(out=st[:, 4, 0:n], in0=st[:, 3, 0:n], in1=st[:, 4, 0:n], op=ALU.subtract))
            tk.op("act", [stid], [stid], lambda e: e.activation(out=st[:, 5, 0:n], in_=st[:, 4, 0:n], func=AF.Ln, bias=EPS, scale=1.0))
            tk.op("act", [stid], [stid], lambda e: e.activation(out=st[:, 6, 0:n], in_=st[:, 5, 0:n], func=AF.Exp, scale=-0.5))
            tk.op("dve", [stid], [stid], lambda e: e.scalar_tensor_tensor(
                out=st[:, 7, 0:n], in0=st[:, 2, 0:n], scalar=-1.0, in1=st[:, 6, 0:n], op0=ALU.mult, op1=ALU.mult))

        def pcc(col):
            return pc[:, col:col + 1]

        for sbi in range(NSB):
            half = sbi % 2
            tok0 = sbi * T
            def load_xT(sb_):
                if sb_ >= NSB:
                    return
                xt_ = xTb[sb_ % 2]
                for hh in range(2):
                    tk.op("pool", [], [("xT", sb_ % 2)], lambda e, hh=hh, sb_=sb_, xt_=xt_: e.dma_start(
                        out=xt_[:, hh * 4:(hh + 1) * 4, :],
                        in_=xT_d[sb_].rearrange("p (k t) -> p k t", k=KC)[:, hh * 4:(hh + 1) * 4, :]), sem=f"xT{sb_ % 2}", inc=16)
            if sbi == 0:
                load_xT(0)
            cur["xT"] = xTb[sbi % 2]
            cur["xTid"] = ("xT", sbi % 2)
            xT = cur["xT"]
            xTid = cur["xTid"]
            uA0 = unit_index[("A", sbi, 0)]
            load_unit(uA0)
            load_unit(uA0 + 1)

            A = {}

            def a_front_a(c):
                load_unit(unit_index[("A", sbi, c)] + 2)
                slot = unit_slot[("A", sbi, c)]
                wid = ("wring", slot)
                hb = hbuf[c % 2]
                hbid = ("hbuf", c % 2)
                mb_ = Mbuf[c % 2]
                mbid = ("Mbuf", c % 2)
                tk.op("sp", [("mscr", c)], [mbid], lambda e: e.dma_start(
                    out=mb_[:].rearrange("p k c -> p (k c)"), in_=mscr_d[c]), sem=f"mb{c % 2}", inc=16)
                if half == 0:
                    tk.op("dve", [], [hbid], lambda e: e.memset(hb[:, 0:PAD], 0.0))
                else:
                    tk.op("dve", [("tail", c)], [hbid], lambda e: e.tensor_copy(out=hb[:, 0:PAD], in_=tail[:, c, :]))
                pg = new_ps()
                proj_job(slot, 0, wid, pg)
                sig, sigid = new_fs()
                tk.op("act", [pg[1], "pc"], [sigid], lambda e: e.activation(
                    out=sig[:], in_=pg[0][:], func=AF.Sigmoid, bias=pcc(PC_BIN + CH_GATE + c), scale=1.0))
                A[c] = {"slot": slot, "wid": wid, "pg": pg, "sig": sig, "sigid": sigid}

            def a_front_v(c):
                a = A[c]
                slot, wid, pg = a["slot"], a["wid"], a["pg"]
                pv = new_ps()
                proj_job(slot, 1, wid, pv)
                a.update(pv=pv)

            def a_front_b(c):
                a = A[c]
                hb = hbuf[c % 2]
                hbid = ("hbuf", c % 2)
                pv, sig, sigid = a["pv"], a["sig"], a["sigid"]
                tk.op("dve", [pv[1], sigid, "pc"], [hbid], lambda e: e.scalar_tensor_tensor(
                    out=hb[:, PAD:PAD + T], in0=pv[0][:], scalar=pcc(PC_BIN + CH_VAL + c), in1=sig[:], op0=ALU.add, op1=ALU.mult))
                if half == 0:
                    tk.op("dve", [hbid], [("tail", c)], lambda e: e.tensor_copy(out=tail[:, c, :], in_=hb[:, T:T + PAD]), force_sync=True)

            def a_taps(c, k0, k1):
                hb = hbuf[c % 2]
                hbid = ("hbuf", c % 2)
                w0 = PC_CW + c * CONV_K
                if k0 == 0:
                    ai = cnt["acc"] % 2
                    cnt["acc"] += 1
                    acc, accid = accp[ai], ("accp", ai)
                    A[c].update(acc=acc, accid=accid)
                    tk.op("dve", [hbid, "pc"], [accid], lambda e: e.tensor_scalar(
                        out=acc[:], in0=hb[:, 0:T], scalar1=pcc(w0), scalar2=None, op0=ALU.mult))
                acc, accid = A[c]["acc"], A[c]["accid"]
                for k in range(max(k0, 1), k1):
                    tk.op("dve", [hbid, "pc", accid], [accid], lambda e, k=k: e.scalar_tensor_tensor(
                        out=acc[:], in0=hb[:, k:k + T], scalar=pcc(w0 + k), in1=acc[:], op0=ALU.mult, op1=ALU.add))

            def a_cast(c):
                a = A[c]
                accb, accbid = new_bs()
                tk.op("act", [a["accid"]], [accbid], lambda e: e.activation(out=accb[:], in_=a["acc"][:], func=AF.Identity))
                a.update(accb=accb, accbid=accbid)

            def a_convA(c):
                a = A[c]
                hb = hbuf[c % 2]
                hbid = ("hbuf", c % 2)
                mb_ = Mbuf[c % 2]
                mbid = ("Mbuf", c % 2)
                pcv = new_ps()
                fns = []
                for k in range(ND, CONV_K):
                    for blk in range(2):
                        fns.append(lambda e, k=k, blk=blk: e.matmul(
                            pcv[0][:, blk * 512:(blk + 1) * 512], mb_[:, k - ND, :], hb[:, blk * 512 + k: blk * 512 + k + 512],
                            start=(k == ND), stop=False))
                tk.op("pe", [hbid, mbid], [pcv[1]], fns)
                a.update(pcv=pcv)

            def a_convB(c):
                a = A[c]
                pcv = a["pcv"]
                fns = []
                for blk in range(2):
                    fns.append(lambda e, blk=blk: e.matmul(
                        pcv[0][:, blk * 512:(blk + 1) * 512], negones_bf[:], a["accb"][:, blk * 512:(blk + 1) * 512],
                        start=False, stop=True))
                tk.op("pe", [a["accbid"], "negones_bf"], [pcv[1]], fns)
                cen, cenid = a["acc"], a["accid"]
                tk.op("dve", [pcv[1], "cbias", cenid], [cenid], lambda e: e.scalar_tensor_tensor(
                    out=cen[:], in0=pcv[0][:], scalar=cbias[:, c:c + 1], in1=cen[:], op0=ALU.add, op1=ALU.add))
                sq, sqid = new_bs()
                tk.op("act", [cenid], [sqid], lambda e: e.activation(out=sq[:], in_=cen[:], func=AF.Square))
                a.update(cen=cen, cenid=cenid, sq=sq, sqid=sqid)

            def a_back_pe(c):
                a = A[c]
                pvar = new_ps()
                tk.op("pe", [a["sqid"], "ones_bf"], [pvar[1]], [
                    (lambda e, blk=blk: e.matmul(pvar[0][:, blk * 512:(blk + 1) * 512], ones_bf[:], a["sq"][:, blk * 512:(blk + 1) * 512], start=True, stop=True))
                    for blk in range(2)])
                std, stdid = new_fs()
                tk.op("act", [pvar[1]], [stdid], lambda e: e.activation(out=std[:], in_=pvar[0][:], func=AF.Ln, bias=EPS, scale=1.0))
                tk.op("act", [stdid], [stdid], lambda e: e.activation(out=std[:], in_=std[:], func=AF.Exp, scale=-0.5))
                gslot = unit_slot[("G", sbi, c)]
                pag = new_ps()
                proj_job(gslot, 0, ("wring", gslot), pag)
                s2, s2id = new_fs()
                tk.op("act", [pag[1], "pc"], [s2id], lambda e: e.activation(
                    out=s2[:], in_=pag[0][:], func=AF.Silu, bias=pcc(PC_BIN + CH_AG + c), scale=1.0))
                a.update(std=std, stdid=stdid, s2=s2, s2id=s2id)

            def a_back_y(c):
                a = A[c]
                std, stdid, cen, cenid = a["std"], a["stdid"], a["cen"], a["cenid"]
                tk.op("dve", [stdid, cenid], [cenid], lambda e: e.tensor_tensor(out=cen[:], in0=cen[:], in1=std[:], op=ALU.mult))
                tk.op("act", [cenid, "pc"], [stdid], lambda e: e.activation(
                    out=std[:], in_=cen[:], func=AF.Silu, bias=pcc(PC_GNB + c), scale=pcc(PC_GNG + c)))

            def a_back_h(c):
                a = A[c]
                std, stdid, s2, s2id = a["std"], a["stdid"], a["s2"], a["s2id"]
                tk.op("dve", [stdid, s2id], [("hA", c)], lambda e: e.tensor_tensor(out=hA[:, c, :], in0=std[:], in1=s2[:], op=ALU.mult))

            V = {}

            VG = 2

            def v_front(g, xT_l=xT):
                st, stid = new_stat()
                V[g] = dict(st=st, stid=stid, vg=[])
                for i in range(VG):
                    tt = g * VG + i
                    pv = new_ps()
                    fns = []
                    for k in range(KC):
                        for hf in range(2):
                            fns.append(lambda e, k=k, hf=hf, tt=tt, pv=pv: e.matmul(
                                pv[0][:, hf * 512:(hf + 1) * 512], xT_l[:, k, tt * 128:(tt + 1) * 128], wres[:, k, hf * 512:(hf + 1) * 512],
                                start=(k == 0), stop=False))
                    for hf in range(2):
                        fns.append(lambda e, hf=hf, pv=pv: e.matmul(
                            pv[0][:, hf * 512:(hf + 1) * 512], ones_row[:], brow[:, hf * 512:(hf + 1) * 512], start=False, stop=True))
                    tk.op("pe", [cur["xTid"], "wres", "ones_row"] + BROW, [pv[1]], fns)
                    vg, vgid = new_fs()
                    tk.op("act", [pv[1]], [vgid, stid], lambda e, pv=pv, vg=vg, i=i: e.activation(
                        out=vg[:], in_=pv[0][:], func=AF.Gelu_apprx_tanh, accum_out=st[:, 0, i:i + 1]))
                    junk, junkid = new_bs()
                    tk.op("act", [vgid, stid], [junkid, stid], lambda e, vg=vg, junk=junk, i=i: e.activation(
                        out=junk[:], in_=vg[:], func=AF.Square, accum_out=st[:, 1, i:i + 1]))
                    V[g]["vg"].append((vg, vgid))

            def v_back(g):
                st, stid = V[g]["st"], V[g]["stid"]
                small_rstd(st, stid, VG, 1.0 / 1024)
                for i in range(VG):
                    tt = g * VG + i
                    vg, vgid = V[g]["vg"][i]
                    tk.op("dve", [vgid, stid], [("big2", tt)], lambda e, vg=vg, i=i, tt=tt: e.tensor_scalar(
                        out=big2[:, tt, :], in0=vg[:], scalar1=st[:, 6, i:i + 1], scalar2=st[:, 7, i:i + 1], op0=ALU.mult, op1=ALU.add))

            Hd = {}

            def h_proj(h):
                load_unit(unit_index[("H", sbi, h)] + 2)
                slot = unit_slot[("H", sbi, h)]
                wid = ("wring", slot)
                pu = new_ps()
                proj_job(slot, 0, wid, pu)
                ug, ugid = new_fs()
                tk.op("act", [pu[1], "pc"], [ugid], lambda e: e.activation(
                    out=ug[:], in_=pu[0][:], func=AF.Gelu_apprx_tanh, bias=pcc(PC_BIN + CH_U + h), scale=1.0))
                pb = new_ps()
                proj_job(slot, 1, wid, pb)
                sg, sgid = new_fs()
                tk.op("act", [pb[1], "pc"], [sgid], lambda e: e.activation(
                    out=sg[:], in_=pb[0][:], func=AF.Silu, bias=pcc(PC_BIN + CH_BG + h), scale=1.0))
                Hd[h] = dict(ug=ug, ugid=ugid, sg=sg, sgid=sgid)

            def h_spatial(h):
                d = Hd[h]
                psp = new_ps()
                tk.op("pe", [("big2", tt) for tt in range(8)] + ["wsbf"], [psp[1]], [
                    (lambda e, tt=tt: e.matmul(psp[0][:, tt * 128:(tt + 1) * 128], big2[:, tt, h * 128:(h + 1) * 128], wsbf[:, h, :], start=True, stop=True))
                    for tt in range(8)])
                vm, vmid = new_fs()
                tk.op("dve", [psp[1], "pc", ("Cm", h)], [vmid], lambda e: e.scalar_tensor_tensor(
                    out=vm[:].rearrange("p (n t) -> p n t", t=128), in0=psp[0][:].rearrange("p (n t) -> p n t", t=128),
                    scalar=pcc(PC_LVG + h), in1=Cm[:, h, :].unsqueeze(1).broadcast_to([128, 8, 128]), op0=ALU.mult, op1=ALU.add))
                tk.op("dve", [vmid, d["ugid"]], [vmid], lambda e: e.tensor_tensor(out=vm[:], in0=vm[:], in1=d["ug"][:], op=ALU.mult))
                tk.op("dve", [vmid, d["sgid"]], [("sB", h)], lambda e: e.tensor_tensor(out=sB[:, h, :], in0=vm[:], in1=d["sg"][:], op=ALU.mult))

            def m_unit(j):
                load_unit(unit_index[("M", sbi, j)] + 2)
                slot = unit_slot[("M", sbi, j)]
                wid = ("wring", slot)
                pga = new_ps()
                proj_job(slot, 0, wid, pga)
                ga, gaid = new_fs()
                tk.op("act", [pga[1], "pc"], [gaid], lambda e: e.activation(
                    out=ga[:], in_=pga[0][:], func=AF.Sigmoid, bias=pcc(PC_BIN + CH_MA + j), scale=1.0))
                pya = new_ps()
                contract_job(slot, 2, wid, hA, [("hA", c) for c in range(8)], pya)
                tk.op("dve", [pya[1], gaid], [gaid], lambda e: e.tensor_tensor(out=ga[:], in0=pya[0][:], in1=ga[:], op=ALU.mult))
                pgb = new_ps()
                proj_job(slot, 1, wid, pgb)
                gb, gbid = new_fs()
                tk.op("act", [pgb[1], "pc"], [gbid], lambda e: e.activation(
                    out=gb[:], in_=pgb[0][:], func=AF.Sigmoid, bias=pcc(PC_BIN + CH_MB + j), scale=1.0))
                pyb = new_ps()
                contract_job(slot, 3, wid, sB, [("sB", h) for h in range(8)], pyb)
                tk.op("dve", [pyb[1], gbid], [gbid], lambda e: e.tensor_tensor(out=gb[:], in0=pyb[0][:], in1=gb[:], op=ALU.mult))
                tk.op("dve", [gaid, gbid], [("big2", j)], lambda e: e.tensor_tensor(out=big2[:, j, :], in0=ga[:], in1=gb[:], op=ALU.add))

            FG = 2
            X = {}
            F = {}

            def x_load(tt, tok0=tok0):
                if tt >= 8 or tt in X:
                    return
                xt_, xid = new_fs()
                r0 = tok0 + tt * 128
                tk.op("sp", [], [xid], lambda e: e.dma_start(out=xt_[:], in_=xtok_d[r0:r0 + 128, :]), sem=f"xl{tt}", inc=16)
                X[tt] = (xt_, xid)

            def f_front(g):
                st, stid = new_stat()
                F[g] = dict(st=st, stid=stid)
                for i in range(FG):
                    tt = g * FG + i
                    x_load(tt)
                    xrt, xid = X[tt]
                    pf = new_ps()
                    fns = []
                    for k in range(KC):
                        for hf in range(2):
                            fns.append(lambda e, k=k, hf=hf, tt=tt, pf=pf: e.matmul(
                                pf[0][:, hf * 512:(hf + 1) * 512], big2[:, k, tt * 128:(tt + 1) * 128], wres[:, k, hf * 512:(hf + 1) * 512],
                                start=(k == 0), stop=False))
                    for hf in range(2):
                        fns.append(lambda e, hf=hf, pf=pf: e.matmul(
                            pf[0][:, hf * 512:(hf + 1) * 512], ones_row[:], brow[:, 1024 + hf * 512:1024 + (hf + 1) * 512], start=False, stop=True))
                    tk.op("pe", [("big2", j) for j in range(8)] + ["wres", "ones_row"] + BROW, [pf[1]], fns)
                    tk.op("dve", [pf[1], xid], [xid], lambda e, xrt=xrt, pf=pf: e.scalar_tensor_tensor(
                        out=xrt[:], in0=xrt[:], scalar=ALPHA, in1=pf[0][:], op0=ALU.mult, op1=ALU.add))
                    j1, j1id = new_bs()
                    tk.op("act", [xid], [j1id, stid], lambda e, xrt=xrt, j1=j1, i=i: e.activation(
                        out=j1[:], in_=xrt[:], func=AF.Identity, accum_out=st[:, 0, i:i + 1]))
                    j2, j2id = new_bs()
                    tk.op("act", [xid, stid], [j2id, stid], lambda e, xrt=xrt, j2=j2, i=i: e.activation(
                        out=j2[:], in_=xrt[:], func=AF.Square, accum_out=st[:, 1, i:i + 1]))

            def f_back(g, tok0=tok0):
                st, stid = F[g]["st"], F[g]["stid"]
                small_rstd(st, stid, FG, 1.0 / 1024)
                for i in range(FG):
                    tt = g * FG + i
                    xrt, xid = X[tt]
                    r0 = tok0 + tt * 128
                    tk.op("act", [xid, stid], [xid], lambda e, xrt=xrt, i=i: e.activation(
                        out=xrt[:], in_=xrt[:], func=AF.Identity, bias=st[:, 7, i:i + 1], scale=st[:, 6, i:i + 1]))
                    tk.op("dve", [xid, "bc"], [xid], lambda e, xrt=xrt: e.tensor_tensor(out=xrt[:], in0=xrt[:], in1=bc[:, 0:1024], op=ALU.mult))
                    tk.op("dve", [xid, "bc"], [xid], lambda e, xrt=xrt: e.tensor_tensor(out=xrt[:], in0=xrt[:], in1=bc[:, 1024:2048], op=ALU.add))
                    tk.op("sp", [xid], [("out", tt, sbi)], lambda e, xrt=xrt, r0=r0: e.dma_start(out=out_d[r0:r0 + 128, :], in_=xrt[:]), sem=f"so{tt}", inc=16)

            T1, T2 = 5, 9
            for it in range(10):
                has_back = it >= 2
                has_conv = 1 <= it <= 8
                if it < 8:
                    a_front_a(it)
                if has_back:
                    a_back_pe(it - 2)
                if it == 8:
                    v_front(0)
                if has_conv:
                    a_taps(it - 1, 0, T1)
                if has_back:
                    a_back_y(it - 2)
                if it < 8:
                    a_front_v(it)
                if has_back:
                    load_unit(unit_index[("G", sbi, it - 2)] + 2)
                if it == 5:
                    load_xT(sbi + 1)
                if it == 3:
                    tk.op("pool", [], ["wres"], lambda e: e.dma_start(out=wres[:], in_=wV_d.rearrange("p (k c) -> p k c", k=KC)), sem="wres", inc=16)
                if has_conv:
                    a_taps(it - 1, T1, T2)
                if has_back:
                    a_back_h(it - 2)
                if has_conv:
                    a_taps(it - 1, T2, ND)
                if it < 8:
                    a_front_b(it)
                if has_conv:
                    a_convA(it - 1)
                    a_cast(it - 1)
                    a_convB(it - 1)
            v_front(1)
            v_back(0)
            v_front(2)
            v_back(1)
            v_front(3)
            v_back(2)
            uH0 = unit_index[("H", sbi, 0)]
            load_unit(uH0)
            load_unit(uH0 + 1)
            h_proj(0)
            v_back(3)
            h_proj(1)
            tk.op("pool", [], ["wres"], lambda e: e.dma_start(out=wres[:], in_=wO_d.rearrange("p (k c) -> p k c", k=KC)), sem="wres", inc=16)
            h_spatial(0)
            for h in range(2, 8):
                h_proj(h)
                h_spatial(h - 1)
            h_spatial(7)
            for j in range(8):
                m_unit(j)
            NG = 8 // FG
            for tt in range(2 * FG):
                x_load(tt)
            f_front(0)
            for g in range(NG):
                if g + 1 < NG:
                    f_front(g + 1)
                for tt in range((g + 2) * FG, (g + 3) * FG):
                    x_load(tt)
                f_back(g)

        def fin(e):
            ins = None
            for s in [f"so{i}" for i in range(8)]:
                e.wait_ge(tk.sems[s], tk.count[s])

        @block.sync
        def _(e):
            tk.replay("sp", e)
            fin(e)

        @block.tensor
        def _(e):
            tk.replay("pe", e)

        @block.scalar
        def _(e):
            tk.replay("act", e)

        @block.vector
        def _(e):
            tk.replay("dve", e)

        @block.gpsimd
        def _(e):
            tk.replay("pool", e)
    return nc


def _prep_weights(inp):
    f = np.float32
    w_in = np.asarray(inp["w_in"], f)
    w_pa = np.asarray(inp["w_pa"], f)
    w_pb = np.asarray(inp["w_pb"], f)
    w_o = np.asarray(inp["w_o"], f)

    def kp(w):
        return w.reshape(KC, 128, -1).transpose(1, 0, 2)

    def cols(w, off, i):
        return kp(w[:, off + i * 128: off + (i + 1) * 128])

    wA = np.stack([np.stack([cols(w_in, 1024, c), cols(w_in, 0, c)], axis=2) for c in range(8)])
    wG = np.stack([cols(w_in, 2048, c) for c in range(8)])
    wB = np.stack([np.stack([cols(w_in, 3072, h), cols(w_in, 5120, h)], axis=2) for h in range(8)])
    wM = np.stack([np.stack([cols(w_in, 6144, j), cols(w_in, 7168, j), cols(w_pa, 0, j), cols(w_pb, 0, j)], axis=2) for j in range(8)])
    wV = kp(w_in[:, 4096:5120])
    wO = kp(w_o)
    pc = np.zeros((128, NPC), f)
    pc[:, PC_BIN:PC_BIN + 64] = np.asarray(inp["b_in"], f).reshape(64, 128).T
    for off, name in ((PC_CB, "conv_b"), (PC_GNG, "gn_g"), (PC_GNB, "gn_b"), (PC_LVG, "ln_v_g"), (PC_LVB, "ln_v_b")):
        pc[:, off:off + 8] = np.asarray(inp[name], f).reshape(8, 128).T
    cw = np.asarray(inp["conv_w"], f)
    pc[:, PC_CW:] = cw.reshape(CONV_K, 8, 128).transpose(2, 1, 0).reshape(128, 8 * CONV_K)
    mats = np.zeros((128, 1280), f)
    mats[:, 0:128] = np.eye(128, dtype=f) - f(1.0 / 128)
    mats[:, 128:256] = np.triu(np.ones((128, 128), f))
    mats[:, 256:] = np.asarray(inp["w_spatial"], f).transpose(2, 0, 1).reshape(128, 1024)
    rows = np.stack([np.asarray(inp["b_in"], f)[4096:5120], np.asarray(inp["b_o"], f)])
    bc = np.concatenate([np.asarray(inp["b_spatial"], f).reshape(-1), np.asarray(inp["ln_out_g"], f), np.asarray(inp["ln_out_b"], f)])
    bc = np.ascontiguousarray(np.broadcast_to(bc[None, :], (128, 3072)))
    c = np.ascontiguousarray
    return dict(wA=c(wA.reshape(8, 128, -1)), wG=c(wG.reshape(8, 128, -1)), wB=c(wB.reshape(8, 128, -1)), wM=c(wM.reshape(8, 128, -1)),
                wV=c(wV.reshape(128, -1)), wO=c(wO.reshape(128, -1)), pc=pc, mats=mats, rows=c(rows), bc=bc)


_NC_CACHE = {}


def kernel(**inputs):
    x = np.asarray(inputs["x"], np.float32)
    shared = _prep_weights(inputs)
    n = 8
    in_maps = []
    for ci in range(n):
        xt = np.ascontiguousarray(x[2 * ci:2 * ci + 2].reshape(NTOK, D))
        xT = np.ascontiguousarray(xt.reshape(NSB, T, KC, 128).transpose(0, 3, 2, 1)).reshape(NSB, 128, KC * T)
        m = dict(shared)
        m["xT"] = xT
        m["xtok"] = xt
        in_maps.append(m)
    if "nc" not in _NC_CACHE:
        _NC_CACHE["nc"] = build_nc()
    nc = _NC_CACHE["nc"]
    res = run_bass_kernel_spmd(nc, in_maps, core_ids=list(range(n)))
    out = np.stack([np.asarray(r["out"], np.float32).reshape(2, 2048, D) for r in res.results]).reshape(16, 2048, D)
    return out
```

```python
import numpy as np
from contextlib import ExitStack
import concourse.bass as bass
import concourse.mybir as mybir
from concourse.bass_utils import run_bass_kernel_spmd

F32 = mybir.dt.float32
BF16 = mybir.dt.bfloat16
AF = mybir.ActivationFunctionType
ALU = mybir.AluOpType

D = 1024
T = 1024
NSB = 4
NTOK = T * NSB
KC = 8
CONV_K = 31
PAD = CONV_K - 1
EPS = 1e-5
ALPHA = 2.0 ** 0.25
ND = 14
NPE = CONV_K - ND

PC_BIN = 0
PC_CB = 64
PC_GNG = 72
PC_GNB = 80
PC_LVG = 88
PC_LVB = 96
PC_CW = 104
NPC = PC_CW + 8 * CONV_K

CH_VAL, CH_GATE, CH_AG, CH_U, CH_V, CH_BG, CH_MA, CH_MB = 0, 8, 16, 24, 32, 40, 48, 56


class Tracker:
    def __init__(self):
        self.last_writer = {}
        self.readers = {}
        self.streams = {k: [] for k in ("pe", "act", "dve", "pool", "sp")}
        self.waited = {k: {} for k in self.streams}
        self.count = {}
        self.sems = {}

    def _deps(self, eng, reads, writes, force_sync=False):
        evs = []
        for b in reads:
            w = self.last_writer.get(b)
            if w is not None:
                evs.append((w, "raw", b))
        for b in writes:
            w = self.last_writer.get(b)
            if w is not None:
                evs.append((w, "waw", b))
            for r in self.readers.get(b, ()):
                evs.append((r, "war", b))
        need = {}
        for (sem, val, prod), kind, buf in evs:
            if prod == eng:
                if eng in ("pe",):
                    continue
                if kind == "war":
                    continue
                if not force_sync and not (isinstance(buf, tuple) and buf[0] == "stat"):
                    continue
            if self.waited[eng].get(sem, 0) >= val:
                continue
            if need.get(sem, 0) < val:
                need[sem] = val
        for sem, val in need.items():
            self.waited[eng][sem] = val
        return list(need.items())

    def _commit(self, reads, writes, ev):
        for b in reads:
            self.readers.setdefault(b, []).append(ev)
        for b in writes:
            self.last_writer[b] = ev
            self.readers[b] = []

    def op(self, eng, reads, writes, fns, sem=None, inc=1, force_sync=False):
        if not isinstance(fns, (list, tuple)):
            fns = [fns]
        waits = self._deps(eng, reads, writes, force_sync)
        semname = sem or eng
        self.count[semname] = self.count.get(semname, 0) + inc
        val = self.count[semname]
        ev = (semname, val, eng if sem is None else "dma")
        self._commit(reads, writes, ev)
        self.streams[eng].append((waits, fns, semname, inc))
        return ev

    def replay(self, eng, e):
        for waits, fns, semname, inc in self.streams[eng]:
            for s, v in waits:
                e.wait_ge(self.sems[s], v)
            ins = None
            for f in fns:
                ins = f(e)
            ins.then_inc(self.sems[semname], inc)


def build_nc():
    nc = bass.Bass("TRN2", target_bir_lowering=False)
    dt = nc.dram_tensor
    xT_d = dt("xT", [NSB, 128, KC * T], F32, kind="ExternalInput").ap()
    xtok_d = dt("xtok", [NTOK, D], F32, kind="ExternalInput").ap()
    wA_d = dt("wA", [8, 128, KC * 2 * 128], F32, kind="ExternalInput").ap()
    wG_d = dt("wG", [8, 128, KC * 1 * 128], F32, kind="ExternalInput").ap()
    wB_d = dt("wB", [8, 128, KC * 2 * 128], F32, kind="ExternalInput").ap()
    wM_d = dt("wM", [8, 128, KC * 4 * 128], F32, kind="ExternalInput").ap()
    wV_d = dt("wV", [128, KC * 1024], F32, kind="ExternalInput").ap()
    wO_d = dt("wO", [128, KC * 1024], F32, kind="ExternalInput").ap()
    pc_d = dt("pc", [128, NPC], F32, kind="ExternalInput").ap()
    mats_d = dt("mats", [128, 1280], F32, kind="ExternalInput").ap()
    rows_d = dt("rows", [2, 1024], F32, kind="ExternalInput").ap()
    bc_d = dt("bc", [128, 3072], F32, kind="ExternalInput").ap()
    out_d = dt("out", [NTOK, D], F32, kind="ExternalOutput").ap()
    mscr_d = dt("mscr", [8, 128, NPE * 128], BF16, kind="Internal").ap()

    tk = Tracker()
    with ExitStack() as es:
        def sb(name, shape, dtype):
            return es.enter_context(nc.sbuf_tensor(name, shape, dtype))

        xTb = [sb(f"xT_sb{i}", [128, KC, T], BF16) for i in range(2)]
        cur = {}
        hA = sb("hA", [128, 8, T], BF16)
        big2 = sb("big2", [128, 8, T], BF16)
        sB = sb("sB", [128, 8, T], BF16)
        NRING = 3
        wring = [sb(f"wring{i}", [128, KC, 4, 128], BF16) for i in range(NRING)]
        wres = sb("wres", [128, KC, 1024], BF16)
        Mbuf = [sb(f"Mbuf{i}", [128, NPE, 128], BF16) for i in range(2)]
        negones_bf = sb("negones_bf", [128, 128], BF16)
        hbuf = [sb(f"hbuf{i}", [128, PAD + T], BF16) for i in range(2)]
        tail = sb("tail", [128, 8, PAD], BF16)
        NF = 8
        fs = [sb(f"fs{i}", [128, T], F32) for i in range(NF)]
        NB = 3
        accp = [sb(f"accp{i}", [128, T], F32) for i in range(2)]
        bs = [sb(f"bs{i}", [128, T], BF16) for i in range(NB)]
        pc = sb("pc_sb", [128, NPC], F32)
        cmt = sb("cm_sb", [128, 128], F32)
        brow = sb("brow", [2, 2048], BF16)
        blo = sb("blo", [1, 2048], BF16)
        bc = sb("bc_sb", [128, 2048], F32)
        Cm = sb("Cm", [128, 8, 128], F32)
        wsbf = sb("wsbf", [128, 8, 128], BF16)
        cbias = sb("cbias", [128, 8], F32)
        ones_bf = sb("ones_bf", [128, 128], BF16)
        onesf = sb("onesf", [128, 128], F32)
        ones_row = sb("ones_row", [2, 128], BF16)
        stat = sb("stat", [128, 8, 8, 4], F32)
        ps = [es.enter_context(nc.psum_tensor(f"ps{i}", [128, T], F32)) for i in range(4)]

        semnames = ["brs", "pe", "act", "dve", "pool", "xT0", "xT1", "wres", "mb0", "mb1", "ms0", "ms1", "c0", "c1", "c2", "c3", "c4", "c5", "c6", "rw0", "rw1"] + [f"xl{i}" for i in range(8)] + [f"so{i}" for i in range(8)] + [f"wr{i}" for i in range(NRING)]
        for s in semnames:
            tk.sems[s] = es.enter_context(nc.semaphore(s))
        block = es.enter_context(nc.Block())

        cnt = {"ps": 0, "fs": 0, "bs": 0, "st": 0, "ring": 0, "acc": 0}

        def new_ps():
            i = cnt["ps"] % 4
            cnt["ps"] += 1
            return ps[i], ("ps", i)

        def new_fs():
            i = cnt["fs"] % NF
            cnt["fs"] += 1
            return fs[i], ("fs", i)

        def new_bs():
            i = cnt["bs"] % NB
            cnt["bs"] += 1
            return bs[i], ("bs", i)

        def new_stat():
            i = cnt["st"] % 8
            cnt["st"] += 1
            return stat[:, i], ("stat", i)

        FS = lambda i: ("fs", i)
        XR = lambda i: ("fs", 6 + i)
        xr = [fs[6], fs[7]]
        wsT_t = fs[3]
        um_t = fs[4]
        wsm_t = fs[5]
        bsp_t = fs[2]
        tk.op("sp", [], ["pc"], lambda e: e.dma_start(out=pc[:], in_=pc_d), sem="c0", inc=16)
        tk.op("sp", [], ["cm"], lambda e: e.dma_start(out=cmt[:], in_=mats_d[:, 0:128]), sem="c1", inc=16)
        tk.op("sp", [], [FS(4)], lambda e: e.dma_start(out=um_t[:, 0:128], in_=mats_d[:, 128:256]), sem="c2", inc=16)
        tk.op("sp", [], [FS(3)], lambda e: e.dma_start(out=wsT_t[:], in_=mats_d[:, 256:1280]), sem="c3", inc=16)
        tk.op("sp", [], [XR(0)], lambda e: e.dma_start(out=xr[0][0:1, :], in_=rows_d[0:1, :]), sem="rw0", inc=16)
        tk.op("sp", [], [XR(1)], lambda e: e.dma_start(out=xr[1][0:1, :], in_=rows_d[1:2, :]), sem="rw1", inc=16)
        tk.op("sp", [], [FS(2)], lambda e: e.dma_start(out=bsp_t[:], in_=bc_d[:, 0:1024]), sem="c5", inc=16)
        tk.op("sp", [], ["bc"], lambda e: e.dma_start(out=bc[:], in_=bc_d[:, 1024:3072]), sem="c6", inc=16)
        tk.op("dve", [], ["ones_bf"], lambda e: e.memset(ones_bf[:], 1.0 / 128))
        tk.op("dve", [], ["onesf"], lambda e: e.memset(onesf[:], 1.0))
        tk.op("dve", [], ["negones_bf"], lambda e: e.memset(negones_bf[:], -1.0 / 128))
        tk.op("dve", [], ["ones_row"], lambda e: e.memset(ones_row[:], 1.0))
        tk.op("dve", [], [("tail", c) for c in range(8)], lambda e: e.memset(tail[:], 0.0))
        cm_ap = cmt[:]
        p0, p0id = new_ps()
        tk.op("pe", ["pc", "onesf"], [p0id], lambda e: e.matmul(p0[:, 0:8], onesf[:], pc[:, PC_CB:PC_CB + 8], start=True, stop=True))
        tk.op("dve", [p0id, "pc"], ["cbias"], lambda e: e.scalar_tensor_tensor(
            out=cbias[:], in0=p0[:, 0:8], scalar=-1.0 / 128, in1=pc[:, PC_CB:PC_CB + 8], op0=ALU.mult, op1=ALU.add))
        tk.op("dve", [FS(3), FS(4)], [FS(5)], lambda e: e.tensor_tensor(
            out=wsm_t[:].rearrange("p (h t) -> p h t", h=8), in0=wsT_t[:].rearrange("p (h t) -> p h t", h=8),
            in1=um_t[:, 0:128].unsqueeze(1).broadcast_to([128, 8, 128]), op=ALU.mult))
        tk.op("dve", [FS(5)], ["wsbf"], lambda e: e.tensor_copy(out=wsbf[:].rearrange("p h t -> p (h t)"), in_=wsm_t[:]))
        p1, p1id = new_ps()
        tk.op("pe", [FS(5), "onesf"], [p1id], [
            (lambda e, j=j: e.matmul(p1[:, j * 512:(j + 1) * 512], onesf[:], wsm_t[:, j * 512:(j + 1) * 512], start=True, stop=True))
            for j in range(2)])
        for h in range(8):
            tk.op("dve", [p1id, "pc", FS(2)], [("Cm", h)], lambda e, h=h: e.scalar_tensor_tensor(
                out=Cm[:, h, :], in0=p1[:, h * 128:(h + 1) * 128], scalar=pc[:, PC_LVB + h:PC_LVB + h + 1],
                in1=bsp_t[:, h * 128:(h + 1) * 128], op0=ALU.mult, op1=ALU.add))
        for r in range(2):
            src = xr[r][0:1, :]
            tmp = fs[r][0:1, :]
            tk.op("dve", [XR(r)], ["brow_hi%d" % r], lambda e, r=r, src=src: e.tensor_copy(out=brow[0:1, r * 1024:(r + 1) * 1024], in_=src))
            tk.op("dve", ["brow_hi%d" % r], [FS(r)], lambda e, r=r, tmp=tmp: e.tensor_copy(out=tmp, in_=brow[0:1, r * 1024:(r + 1) * 1024]))
            tk.op("dve", [FS(r), XR(r)], [FS(r)], lambda e, src=src, tmp=tmp: e.tensor_tensor(out=tmp, in0=src, in1=tmp, op=ALU.subtract))
            tk.op("dve", [FS(r)], ["blo%d" % r], lambda e, r=r, tmp=tmp: e.tensor_copy(out=blo[:, r * 1024:(r + 1) * 1024], in_=tmp))
        tk.op("sp", ["blo0", "blo1"], ["brow_lo"], lambda e: e.dma_start(out=brow[1:2, :], in_=blo[:]), sem="brs", inc=16)
        BROW = ["brow_hi0", "brow_hi1", "brow_lo"]
        for c in range(8):
            mb_ = Mbuf[c % 2]
            tk.op("dve", ["cm", "pc"], [("Mbuf", c % 2)], lambda e, c=c, mb_=mb_: e.tensor_tensor(
                out=mb_[:], in0=cm_ap.unsqueeze(1).broadcast_to([128, NPE, 128]),
                in1=pc[:, PC_CW + c * CONV_K + ND:PC_CW + (c + 1) * CONV_K].unsqueeze(2).broadcast_to([128, NPE, 128]), op=ALU.mult))
            tk.op("sp", [("Mbuf", c % 2)], [("mscr", c)], lambda e, c=c, mb_=mb_: e.dma_start(
                out=mscr_d[c], in_=mb_[:].rearrange("p k c -> p (k c)")), sem=f"ms{c % 2}", inc=16)

        units = []
        for sbi in range(NSB):
            units.append(("A", sbi, 0))
            units.append(("A", sbi, 1))
            for c in range(2, 8):
                units.append(("A", sbi, c))
                units.append(("G", sbi, c - 2))
            units.append(("G", sbi, 6))
            units.append(("G", sbi, 7))
            for h in range(8):
                units.append(("H", sbi, h))
            for j in range(8):
                units.append(("M", sbi, j))
        unit_slot = {}
        loaded = set()

        def load_unit(n):
            if n >= len(units) or n in loaded:
                return
            loaded.add(n)
            kind, sbi, i = units[n]
            slot = cnt["ring"] % NRING
            cnt["ring"] += 1
            unit_slot[(kind, sbi, i)] = slot
            ns = {"A": 2, "G": 1, "H": 2, "M": 4}[kind]
            src = {"A": wA_d, "G": wG_d, "H": wB_d, "M": wM_d}[kind][i]
            dst = wring[slot][:, :, 0:ns, :]
            tk.op("pool", [], [("wring", slot)], lambda e: e.dma_start(
                out=dst, in_=src.rearrange("p (k s c) -> p k s c", k=KC, s=ns)), sem=f"wr{slot}", inc=16)

        unit_index = {u: n for n, u in enumerate(units)}

        def proj_job(wslot, s, wid, psid_t, extra_reads=()):
            pst, psid = psid_t
            xT = cur["xT"]
            fns = []
            for k in range(KC):
                for blk in range(2):
                    fns.append(lambda e, k=k, blk=blk: e.matmul(
                        pst[:, blk * 512:(blk + 1) * 512], wring[wslot][:, k, s, :], xT[:, k, blk * 512:(blk + 1) * 512],
                        start=(k == 0), stop=(k == KC - 1)))
            tk.op("pe", [wid, cur["xTid"]] + list(extra_reads), [psid], fns)

        def contract_job(wslot, s, wid, src, srcids, psid_t):
            pst, psid = psid_t
            fns = []
            for k in range(KC):
                for blk in range(2):
                    fns.append(lambda e, k=k, blk=blk: e.matmul(
                        pst[:, blk * 512:(blk + 1) * 512], wring[wslot][:, k, s, :], src[:, k, blk * 512:(blk + 1) * 512],
                        start=(k == 0), stop=(k == KC - 1)))
            tk.op("pe", [wid] + list(srcids), [psid], fns)

        def small_rstd(st, stid, n, n_inv):
            tk.op("dve", [stid], [stid], lambda e: e.tensor_scalar(out=st[:, 2:4, 0:n], in0=st[:, 0:2, 0:n], scalar1=n_inv, scalar2=None, op0=ALU.mult))
            tk.op("dve", [stid], [stid], lambda e: e.tensor_tensor(out=st[:, 4, 0:n], in0=st[:, 2, 0:n], in1=st[:, 2, 0:n], op=ALU.mult))
            tk.op("dve", [stid], [stid], lambda e: e.tensor_tensor(out=st[:, 4, 0:n], in0=st[:, 3, 0:n], in1=st[:, 4, 0:n], op=ALU.subtract))
            tk.op("act", [stid], [stid], lambda e: e.activation(out=st[:, 5, 0:n], in_=st[:, 4, 0:n], func=AF.Ln, bias=EPS, scale=1.0))
            tk.op("act", [stid], [stid], lambda e: e.activation(out=st[:, 6, 0:n], in_=st[:, 5, 0:n], func=AF.Exp, scale=-0.5))
            tk.op("dve", [stid], [stid], lambda e: e.scalar_tensor_tensor(
                out=st[:, 7, 0:n], in0=st[:, 2, 0:n], scalar=-1.0, in1=st[:, 6, 0:n], op0=ALU.mult, op1=ALU.mult))

        def pcc(col):
            return pc[:, col:col + 1]

        for sbi in range(NSB):
            half = sbi % 2
            tok0 = sbi * T
            def load_xT(sb_):
                if sb_ >= NSB:
                    return
                xt_ = xTb[sb_ % 2]
                for hh in range(2):
                    tk.op("pool", [], [("xT", sb_ % 2)], lambda e, hh=hh, sb_=sb_, xt_=xt_: e.dma_start(
                        out=xt_[:, hh * 4:(hh + 1) * 4, :],
                        in_=xT_d[sb_].rearrange("p (k t) -> p k t", k=KC)[:, hh * 4:(hh + 1) * 4, :]), sem=f"xT{sb_ % 2}", inc=16)
            if sbi == 0:
                load_xT(0)
            cur["xT"] = xTb[sbi % 2]
            cur["xTid"] = ("xT", sbi % 2)
            xT = cur["xT"]
            xTid = cur["xTid"]
            uA0 = unit_index[("A", sbi, 0)]
            load_unit(uA0)
            load_unit(uA0 + 1)

            A = {}

            def a_front_a(c):
                load_unit(unit_index[("A", sbi, c)] + 2)
                slot = unit_slot[("A", sbi, c)]
                wid = ("wring", slot)
                hb = hbuf[c % 2]
                hbid = ("hbuf", c % 2)
                mb_ = Mbuf[c % 2]
                mbid = ("Mbuf", c % 2)
                tk.op("sp", [("mscr", c)], [mbid], lambda e: e.dma_start(
                    out=mb_[:].rearrange("p k c -> p (k c)"), in_=mscr_d[c]), sem=f"mb{c % 2}", inc=16)
                if half == 0:
                    tk.op("dve", [], [hbid], lambda e: e.memset(hb[:, 0:PAD], 0.0))
                else:
                    tk.op("dve", [("tail", c)], [hbid], lambda e: e.tensor_copy(out=hb[:, 0:PAD], in_=tail[:, c, :]))
                pg = new_ps()
                proj_job(slot, 0, wid, pg)
                sig, sigid = new_fs()
                tk.op("act", [pg[1], "pc"], [sigid], lambda e: e.activation(
                    out=sig[:], in_=pg[0][:], func=AF.Sigmoid, bias=pcc(PC_BIN + CH_GATE + c), scale=1.0))
                A[c] = {"slot": slot, "wid": wid, "pg": pg, "sig": sig, "sigid": sigid}

            def a_front_v(c):
                a = A[c]
                slot, wid, pg = a["slot"], a["wid"], a["pg"]
                pv = new_ps()
                proj_job(slot, 1, wid, pv)
                a.update(pv=pv)

            def a_front_b(c):
                a = A[c]
                hb = hbuf[c % 2]
                hbid = ("hbuf", c % 2)
                pv, sig, sigid = a["pv"], a["sig"], a["sigid"]
                tk.op("dve", [pv[1], sigid, "pc"], [hbid], lambda e: e.scalar_tensor_tensor(
                    out=hb[:, PAD:PAD + T], in0=pv[0][:], scalar=pcc(PC_BIN + CH_VAL + c), in1=sig[:], op0=ALU.add, op1=ALU.mult))
                if half == 0:
                    tk.op("dve", [hbid], [("tail", c)], lambda e: e.tensor_copy(out=tail[:, c, :], in_=hb[:, T:T + PAD]), force_sync=True)

            def a_taps(c, k0, k1):
                hb = hbuf[c % 2]
                hbid = ("hbuf", c % 2)
                w0 = PC_CW + c * CONV_K
                if k0 == 0:
                    ai = cnt["acc"] % 2
                    cnt["acc"] += 1
                    acc, accid = accp[ai], ("accp", ai)
                    A[c].update(acc=acc, accid=accid)
                    tk.op("dve", [hbid, "pc"], [accid], lambda e: e.tensor_scalar(
                        out=acc[:], in0=hb[:, 0:T], scalar1=pcc(w0), scalar2=None, op0=ALU.mult))
                acc, accid = A[c]["acc"], A[c]["accid"]
                for k in range(max(k0, 1), k1):
                    tk.op("dve", [hbid, "pc", accid], [accid], lambda e, k=k: e.scalar_tensor_tensor(
                        out=acc[:], in0=hb[:, k:k + T], scalar=pcc(w0 + k), in1=acc[:], op0=ALU.mult, op1=ALU.add))

            def a_cast(c):
                a = A[c]
                accb, accbid = new_bs()
                tk.op("act", [a["accid"]], [accbid], lambda e: e.activation(out=accb[:], in_=a["acc"][:], func=AF.Identity))
                a.update(accb=accb, accbid=accbid)

            def a_convA(c):
                a = A[c]
                hb = hbuf[c % 2]
                hbid = ("hbuf", c % 2)
                mb_ = Mbuf[c % 2]
                mbid = ("Mbuf", c % 2)
                pcv = new_ps()
                fns = []
                for k in range(ND, CONV_K):
                    for blk in range(2):
                        fns.append(lambda e, k=k, blk=blk: e.matmul(
                            pcv[0][:, blk * 512:(blk + 1) * 512], mb_[:, k - ND, :], hb[:, blk * 512 + k: blk * 512 + k + 512],
                            start=(k == ND), stop=False))
                tk.op("pe", [hbid, mbid], [pcv[1]], fns)
                a.update(pcv=pcv)

            def a_convB(c):
                a = A[c]
                pcv = a["pcv"]
                fns = []
                for blk in range(2):
                    fns.append(lambda e, blk=blk: e.matmul(
                        pcv[0][:, blk * 512:(blk + 1) * 512], negones_bf[:], a["accb"][:, blk * 512:(blk + 1) * 512],
                        start=False, stop=True))
                tk.op("pe", [a["accbid"], "negones_bf"], [pcv[1]], fns)
                cen, cenid = a["acc"], a["accid"]
                tk.op("dve", [pcv[1], "cbias", cenid], [cenid], lambda e: e.scalar_tensor_tensor(
                    out=cen[:], in0=pcv[0][:], scalar=cbias[:, c:c + 1], in1=cen[:], op0=ALU.add, op1=ALU.add))
                sq, sqid = new_bs()
                tk.op("act", [cenid], [sqid], lambda e: e.activation(out=sq[:], in_=cen[:], func=AF.Square))
                a.update(cen=cen, cenid=cenid, sq=sq, sqid=sqid)

            def a_back_pe(c):
                a = A[c]
                pvar = new_ps()
                tk.op("pe", [a["sqid"], "ones_bf"], [pvar[1]], [
                    (lambda e, blk=blk: e.matmul(pvar[0][:, blk * 512:(blk + 1) * 512], ones_bf[:], a["sq"][:, blk * 512:(blk + 1) * 512], start=True, stop=True))
                    for blk in range(2)])
                std, stdid = new_fs()
                tk.op("act", [pvar[1]], [stdid], lambda e: e.activation(out=std[:], in_=pvar[0][:], func=AF.Ln, bias=EPS, scale=1.0))
                tk.op("act", [stdid], [stdid], lambda e: e.activation(out=std[:], in_=std[:], func=AF.Exp, scale=-0.5))
                gslot = unit_slot[("G", sbi, c)]
                pag = new_ps()
                proj_job(gslot, 0, ("wring", gslot), pag)
                s2, s2id = new_fs()
                tk.op("act", [pag[1], "pc"], [s2id], lambda e: e.activation(
                    out=s2[:], in_=pag[0][:], func=AF.Silu, bias=pcc(PC_BIN + CH_AG + c), scale=1.0))
                a.update(std=std, stdid=stdid, s2=s2, s2id=s2id)

            def a_back_y(c):
                a = A[c]
                std, stdid, cen, cenid = a["std"], a["stdid"], a["cen"], a["cenid"]
                tk.op("dve", [stdid, cenid], [cenid], lambda e: e.tensor_tensor(out=cen[:], in0=cen[:], in1=std[:], op=ALU.mult))
                tk.op("act", [cenid, "pc"], [stdid], lambda e: e.activation(
                    out=std[:], in_=cen[:], func=AF.Silu, bias=pcc(PC_GNB + c), scale=pcc(PC_GNG + c)))

            def a_back_h(c):
                a = A[c]
                std, stdid, s2, s2id = a["std"], a["stdid"], a["s2"], a["s2id"]
                tk.op("dve", [stdid, s2id], [("hA", c)], lambda e: e.tensor_tensor(out=hA[:, c, :], in0=std[:], in1=s2[:], op=ALU.mult))

            V = {}

            VG = 2

            def v_front(g, xT_l=xT):
                st, stid = new_stat()
                V[g] = dict(st=st, stid=stid, vg=[])
                for i in range(VG):
                    tt = g * VG + i
                    pv = new_ps()
                    fns = []
                    for k in range(KC):
                        for hf in range(2):
                            fns.append(lambda e, k=k, hf=hf, tt=tt, pv=pv: e.matmul(
                                pv[0][:, hf * 512:(hf + 1) * 512], xT_l[:, k, tt * 128:(tt + 1) * 128], wres[:, k, hf * 512:(hf + 1) * 512],
                                start=(k == 0), stop=False))
                    for hf in range(2):
                        fns.append(lambda e, hf=hf, pv=pv: e.matmul(
                            pv[0][:, hf * 512:(hf + 1) * 512], ones_row[:], brow[:, hf * 512:(hf + 1) * 512], start=False, stop=True))
                    tk.op("pe", [cur["xTid"], "wres", "ones_row"] + BROW, [pv[1]], fns)
                    vg, vgid = new_fs()
                    tk.op("act", [pv[1]], [vgid, stid], lambda e, pv=pv, vg=vg, i=i: e.activation(
                        out=vg[:], in_=pv[0][:], func=AF.Gelu_apprx_tanh, accum_out=st[:, 0, i:i + 1]))
                    junk, junkid = new_bs()
                    tk.op("act", [vgid, stid], [junkid, stid], lambda e, vg=vg, junk=junk, i=i: e.activation(
                        out=junk[:], in_=vg[:], func=AF.Square, accum_out=st[:, 1, i:i + 1]))
                    V[g]["vg"].append((vg, vgid))

            def v_back(g):
                st, stid = V[g]["st"], V[g]["stid"]
                small_rstd(st, stid, VG, 1.0 / 1024)
                for i in range(VG):
                    tt = g * VG + i
                    vg, vgid = V[g]["vg"][i]
                    tk.op("dve", [vgid, stid], [("big2", tt)], lambda e, vg=vg, i=i, tt=tt: e.tensor_scalar(
                        out=big2[:, tt, :], in0=vg[:], scalar1=st[:, 6, i:i + 1], scalar2=st[:, 7, i:i + 1], op0=ALU.mult, op1=ALU.add))

            Hd = {}

            def h_proj(h):
                load_unit(unit_index[("H", sbi, h)] + 2)
                slot = unit_slot[("H", sbi, h)]
                wid = ("wring", slot)
                pu = new_ps()
                proj_job(slot, 0, wid, pu)
                ug, ugid = new_fs()
                tk.op("act", [pu[1], "pc"], [ugid], lambda e: e.activation(
                    out=ug[:], in_=pu[0][:], func=AF.Gelu_apprx_tanh, bias=pcc(PC_BIN + CH_U + h), scale=1.0))
                pb = new_ps()
                proj_job(slot, 1, wid, pb)
                sg, sgid = new_fs()
                tk.op("act", [pb[1], "pc"], [sgid], lambda e: e.activation(
                    out=sg[:], in_=pb[0][:], func=AF.Silu, bias=pcc(PC_BIN + CH_BG + h), scale=1.0))
                Hd[h] = dict(ug=ug, ugid=ugid, sg=sg, sgid=sgid)

            def h_spatial(h):
                d = Hd[h]
                psp = new_ps()
                tk.op("pe", [("big2", tt) for tt in range(8)] + ["wsbf"], [psp[1]], [
                    (lambda e, tt=tt: e.matmul(psp[0][:, tt * 128:(tt + 1) * 128], big2[:, tt, h * 128:(h + 1) * 128], wsbf[:, h, :], start=True, stop=True))
                    for tt in range(8)])
                vm, vmid = new_fs()
                tk.op("dve", [psp[1], "pc", ("Cm", h)], [vmid], lambda e: e.scalar_tensor_tensor(
                    out=vm[:].rearrange("p (n t) -> p n t", t=128), in0=psp[0][:].rearrange("p (n t) -> p n t", t=128),
                    scalar=pcc(PC_LVG + h), in1=Cm[:, h, :].unsqueeze(1).broadcast_to([128, 8, 128]), op0=ALU.mult, op1=ALU.add))
                tk.op("dve", [vmid, d["ugid"]], [vmid], lambda e: e.tensor_tensor(out=vm[:], in0=vm[:], in1=d["ug"][:], op=ALU.mult))
                tk.op("dve", [vmid, d["sgid"]], [("sB", h)], lambda e: e.tensor_tensor(out=sB[:, h, :], in0=vm[:], in1=d["sg"][:], op=ALU.mult))

            def m_unit(j):
                load_unit(unit_index[("M", sbi, j)] + 2)
                slot = unit_slot[("M", sbi, j)]
                wid = ("wring", slot)
                pga = new_ps()
                proj_job(slot, 0, wid, pga)
                ga, gaid = new_fs()
                tk.op("act", [pga[1], "pc"], [gaid], lambda e: e.activation(
                    out=ga[:], in_=pga[0][:], func=AF.Sigmoid, bias=pcc(PC_BIN + CH_MA + j), scale=1.0))
                pya = new_ps()
                contract_job(slot, 2, wid, hA, [("hA", c) for c in range(8)], pya)
                tk.op("dve", [pya[1], gaid], [gaid], lambda e: e.tensor_tensor(out=ga[:], in0=pya[0][:], in1=ga[:], op=ALU.mult))
                pgb = new_ps()
                proj_job(slot, 1, wid, pgb)
                gb, gbid = new_fs()
                tk.op("act", [pgb[1], "pc"], [gbid], lambda e: e.activation(
                    out=gb[:], in_=pgb[0][:], func=AF.Sigmoid, bias=pcc(PC_BIN + CH_MB + j), scale=1.0))
                pyb = new_ps()
                contract_job(slot, 3, wid, sB, [("sB", h) for h in range(8)], pyb)
                tk.op("dve", [pyb[1], gbid], [gbid], lambda e: e.tensor_tensor(out=gb[:], in0=pyb[0][:], in1=gb[:], op=ALU.mult))
                tk.op("dve", [gaid, gbid], [("big2", j)], lambda e: e.tensor_tensor(out=big2[:, j, :], in0=ga[:], in1=gb[:], op=ALU.add))

            FG = 2
            X = {}
            F = {}

            def x_load(tt, tok0=tok0):
                if tt >= 8 or tt in X:
                    return
                xt_, xid = new_fs()
                r0 = tok0 + tt * 128
                tk.op("sp", [], [xid], lambda e: e.dma_start(out=xt_[:], in_=xtok_d[r0:r0 + 128, :]), sem=f"xl{tt}", inc=16)
                X[tt] = (xt_, xid)

            def f_front(g):
                st, stid = new_stat()
                F[g] = dict(st=st, stid=stid)
                for i in range(FG):
                    tt = g * FG + i
                    x_load(tt)
                    xrt, xid = X[tt]
                    pf = new_ps()
                    fns = []
                    for k in range(KC):
                        for hf in range(2):
                            fns.append(lambda e, k=k, hf=hf, tt=tt, pf=pf: e.matmul(
                                pf[0][:, hf * 512:(hf + 1) * 512], big2[:, k, tt * 128:(tt + 1) * 128], wres[:, k, hf * 512:(hf + 1) * 512],
                                start=(k == 0), stop=False))
                    for hf in range(2):
                        fns.append(lambda e, hf=hf, pf=pf: e.matmul(
                            pf[0][:, hf * 512:(hf + 1) * 512], ones_row[:], brow[:, 1024 + hf * 512:1024 + (hf + 1) * 512], start=False, stop=True))
                    tk.op("pe", [("big2", j) for j in range(8)] + ["wres", "ones_row"] + BROW, [pf[1]], fns)
                    tk.op("dve", [pf[1], xid], [xid], lambda e, xrt=xrt, pf=pf: e.scalar_tensor_tensor(
                        out=xrt[:], in0=xrt[:], scalar=ALPHA, in1=pf[0][:], op0=ALU.mult, op1=ALU.add))
                    j1, j1id = new_bs()
                    tk.op("act", [xid], [j1id, stid], lambda e, xrt=xrt, j1=j1, i=i: e.activation(
                        out=j1[:], in_=xrt[:], func=AF.Identity, accum_out=st[:, 0, i:i + 1]))
                    j2, j2id = new_bs()
                    tk.op("act", [xid, stid], [j2id, stid], lambda e, xrt=xrt, j2=j2, i=i: e.activation(
                        out=j2[:], in_=xrt[:], func=AF.Square, accum_out=st[:, 1, i:i + 1]))

            def f_back(g, tok0=tok0):
                st, stid = F[g]["st"], F[g]["stid"]
                small_rstd(st, stid, FG, 1.0 / 1024)
                for i in range(FG):
                    tt = g * FG + i
                    xrt, xid = X[tt]
                    r0 = tok0 + tt * 128
                    tk.op("act", [xid, stid], [xid], lambda e, xrt=xrt, i=i: e.activation(
                        out=xrt[:], in_=xrt[:], func=AF.Identity, bias=st[:, 7, i:i + 1], scale=st[:, 6, i:i + 1]))
                    tk.op("dve", [xid, "bc"], [xid], lambda e, xrt=xrt: e.tensor_tensor(out=xrt[:], in0=xrt[:], in1=bc[:, 0:1024], op=ALU.mult))
                    tk.op("dve", [xid, "bc"], [xid], lambda e, xrt=xrt: e.tensor_tensor(out=xrt[:], in0=xrt[:], in1=bc[:, 1024:2048], op=ALU.add))
                    tk.op("sp", [xid], [("out", tt, sbi)], lambda e, xrt=xrt, r0=r0: e.dma_start(out=out_d[r0:r0 + 128, :], in_=xrt[:]), sem=f"so{tt}", inc=16)

            T1, T2 = 5, 9
            for it in range(10):
                has_back = it >= 2
                has_conv = 1 <= it <= 8
                if it < 8:
                    a_front_a(it)
                if has_back:
                    a_back_pe(it - 2)
                if it == 8:
                    v_front(0)
                if has_conv:
                    a_taps(it - 1, 0, T1)
                if has_back:
                    a_back_y(it - 2)
                if it < 8:
                    a_front_v(it)
                if has_back:
                    load_unit(unit_index[("G", sbi, it - 2)] + 2)
                if it == 5:
                    load_xT(sbi + 1)
                if it == 3:
                    tk.op("pool", [], ["wres"], lambda e: e.dma_start(out=wres[:], in_=wV_d.rearrange("p (k c) -> p k c", k=KC)), sem="wres", inc=16)
                if has_conv:
                    a_taps(it - 1, T1, T2)
                if has_back:
                    a_back_h(it - 2)
                if has_conv:
                    a_taps(it - 1, T2, ND)
                if it < 8:
                    a_front_b(it)
                if has_conv:
                    a_convA(it - 1)
                    a_cast(it - 1)
                    a_convB(it - 1)
            v_front(1)
            v_back(0)
            v_front(2)
            v_back(1)
            v_front(3)
            v_back(2)
            uH0 = unit_index[("H", sbi, 0)]
            load_unit(uH0)
            load_unit(uH0 + 1)
            h_proj(0)
            v_back(3)
            h_proj(1)
            tk.op("pool", [], ["wres"], lambda e: e.dma_start(out=wres[:], in_=wO_d.rearrange("p (k c) -> p k c", k=KC)), sem="wres", inc=16)
            h_spatial(0)
            for h in range(2, 8):
                h_proj(h)
                h_spatial(h - 1)
            h_spatial(7)
            for j in range(8):
                m_unit(j)
            NG = 8 // FG
            for tt in range(2 * FG):
                x_load(tt)
            f_front(0)
            for g in range(NG):
                if g + 1 < NG:
                    f_front(g + 1)
                for tt in range((g + 2) * FG, (g + 3) * FG):
                    x_load(tt)
                f_back(g)

        def fin(e):
            ins = None
            for s in [f"so{i}" for i in range(8)]:
                e.wait_ge(tk.sems[s], tk.count[s])

        @block.sync
        def _(e):
            tk.replay("sp", e)
            fin(e)

        @block.tensor
        def _(e):
            tk.replay("pe", e)

        @block.scalar
        def _(e):
            tk.replay("act", e)

        @block.vector
        def _(e):
            tk.replay("dve", e)

        @block.gpsimd
        def _(e):
            tk.replay("pool", e)
    return nc


def _prep_weights(inp):
    f = np.float32
    w_in = np.asarray(inp["w_in"], f)
    w_pa = np.asarray(inp["w_pa"], f)
    w_pb = np.asarray(inp["w_pb"], f)
    w_o = np.asarray(inp["w_o"], f)

    def kp(w):
        return w.reshape(KC, 128, -1).transpose(1, 0, 2)

    def cols(w, off, i):
        return kp(w[:, off + i * 128: off + (i + 1) * 128])

    wA = np.stack([np.stack([cols(w_in, 1024, c), cols(w_in, 0, c)], axis=2) for c in range(8)])
    wG = np.stack([cols(w_in, 2048, c) for c in range(8)])
    wB = np.stack([np.stack([cols(w_in, 3072, h), cols(w_in, 5120, h)], axis=2) for h in range(8)])
    wM = np.stack([np.stack([cols(w_in, 6144, j), cols(w_in, 7168, j), cols(w_pa, 0, j), cols(w_pb, 0, j)], axis=2) for j in range(8)])
    wV = kp(w_in[:, 4096:5120])
    wO = kp(w_o)
    pc = np.zeros((128, NPC), f)
    pc[:, PC_BIN:PC_BIN + 64] = np.asarray(inp["b_in"], f).reshape(64, 128).T
    for off, name in ((PC_CB, "conv_b"), (PC_GNG, "gn_g"), (PC_GNB, "gn_b"), (PC_LVG, "ln_v_g"), (PC_LVB, "ln_v_b")):
        pc[:, off:off + 8] = np.asarray(inp[name], f).reshape(8, 128).T
    cw = np.asarray(inp["conv_w"], f)
    pc[:, PC_CW:] = cw.reshape(CONV_K, 8, 128).transpose(2, 1, 0).reshape(128, 8 * CONV_K)
    mats = np.zeros((128, 1280), f)
    mats[:, 0:128] = np.eye(128, dtype=f) - f(1.0 / 128)
    mats[:, 128:256] = np.triu(np.ones((128, 128), f))
    mats[:, 256:] = np.asarray(inp["w_spatial"], f).transpose(2, 0, 1).reshape(128, 1024)
    rows = np.stack([np.asarray(inp["b_in"], f)[4096:5120], np.asarray(inp["b_o"], f)])
    bc = np.concatenate([np.asarray(inp["b_spatial"], f).reshape(-1), np.asarray(inp["ln_out_g"], f), np.asarray(inp["ln_out_b"], f)])
    bc = np.ascontiguousarray(np.broadcast_to(bc[None, :], (128, 3072)))
    c = np.ascontiguousarray
    return dict(wA=c(wA.reshape(8, 128, -1)), wG=c(wG.reshape(8, 128, -1)), wB=c(wB.reshape(8, 128, -1)), wM=c(wM.reshape(8, 128, -1)),
                wV=c(wV.reshape(128, -1)), wO=c(wO.reshape(128, -1)), pc=pc, mats=mats, rows=c(rows), bc=bc)


_NC_CACHE = {}


def kernel(**inputs):
    x = np.asarray(inputs["x"], np.float32)
    shared = _prep_weights(inputs)
    n = 8
    in_maps = []
    for ci in range(n):
        xt = np.ascontiguousarray(x[2 * ci:2 * ci + 2].reshape(NTOK, D))
        xT = np.ascontiguousarray(xt.reshape(NSB, T, KC, 128).transpose(0, 3, 2, 1)).reshape(NSB, 128, KC * T)
        m = dict(shared)
        m["xT"] = xT
        m["xtok"] = xt
        in_maps.append(m)
    if "nc" not in _NC_CACHE:
        _NC_CACHE["nc"] = build_nc()
    nc = _NC_CACHE["nc"]
    res = run_bass_kernel_spmd(nc, in_maps, core_ids=list(range(n)))
    out = np.stack([np.asarray(r["out"], np.float32).reshape(2, 2048, D) for r in res.results]).reshape(16, 2048, D)
    return out
```

```python
import numpy as np
from contextlib import ExitStack
import concourse.bass as bass
import concourse.mybir as mybir
from concourse.bass_utils import run_bass_kernel_spmd

F32 = mybir.dt.float32
BF16 = mybir.dt.bfloat16
AF = mybir.ActivationFunctionType
ALU = mybir.AluOpType

D = 1024
T = 1024
NSB = 4
NTOK = T * NSB
KC = 8
CONV_K = 31
PAD = CONV_K - 1
EPS = 1e-5
ALPHA = 2.0 ** 0.25
ND = 13
NPE = CONV_K - ND

PC_BIN = 0
PC_CB = 64
PC_GNG = 72
PC_GNB = 80
PC_LVG = 88
PC_LVB = 96
PC_CW = 104
NPC = PC_CW + 8 * CONV_K

CH_VAL, CH_GATE, CH_AG, CH_U, CH_V, CH_BG, CH_MA, CH_MB = 0, 8, 16, 24, 32, 40, 48, 56


class Tracker:
    def __init__(self):
        self.last_writer = {}
        self.readers = {}
        self.streams = {k: [] for k in ("pe", "act", "dve", "pool", "sp")}
        self.waited = {k: {} for k in self.streams}
        self.count = {}
        self.sems = {}

    def _deps(self, eng, reads, writes, force_sync=False):
        evs = []
        for b in reads:
            w = self.last_writer.get(b)
            if w is not None:
                evs.append((w, "raw", b))
        for b in writes:
            w = self.last_writer.get(b)
            if w is not None:
                evs.append((w, "waw", b))
            for r in self.readers.get(b, ()):
                evs.append((r, "war", b))
        need = {}
        for (sem, val, prod), kind, buf in evs:
            if prod == eng:
                if eng in ("pe",):
                    continue
                if kind == "war":
                    continue
                if not force_sync and not (isinstance(buf, tuple) and buf[0] == "stat"):
                    continue
            if self.waited[eng].get(sem, 0) >= val:
                continue
            if need.get(sem, 0) < val:
                need[sem] = val
        for sem, val in need.items():
            self.waited[eng][sem] = val
        return list(need.items())

    def _commit(self, reads, writes, ev):
        for b in reads:
            self.readers.setdefault(b, []).append(ev)
        for b in writes:
            self.last_writer[b] = ev
            self.readers[b] = []

    def op(self, eng, reads, writes, fns, sem=None, inc=1, force_sync=False):
        if not isinstance(fns, (list, tuple)):
            fns = [fns]
        waits = self._deps(eng, reads, writes, force_sync)
        semname = sem or eng
        self.count[semname] = self.count.get(semname, 0) + inc
        val = self.count[semname]
        ev = (semname, val, eng if sem is None else "dma")
        self._commit(reads, writes, ev)
        self.streams[eng].append((waits, fns, semname, inc))
        return ev

    def replay(self, eng, e):
        for waits, fns, semname, inc in self.streams[eng]:
            for s, v in waits:
                e.wait_ge(self.sems[s], v)
            ins = None
            for f in fns:
                ins = f(e)
            ins.then_inc(self.sems[semname], inc)


def build_nc():
    nc = bass.Bass("TRN2", target_bir_lowering=False)
    dt = nc.dram_tensor
    xT_d = dt("xT", [NSB, 128, KC * T], F32, kind="ExternalInput").ap()
    xtok_d = dt("xtok", [NTOK, D], F32, kind="ExternalInput").ap()
    wA_d = dt("wA", [8, 128, KC * 2 * 128], F32, kind="ExternalInput").ap()
    wG_d = dt("wG", [8, 128, KC * 1 * 128], F32, kind="ExternalInput").ap()
    wB_d = dt("wB", [8, 128, KC * 2 * 128], F32, kind="ExternalInput").ap()
    wM_d = dt("wM", [8, 128, KC * 4 * 128], F32, kind="ExternalInput").ap()
    wV_d = dt("wV", [128, KC * 1024], F32, kind="ExternalInput").ap()
    wO_d = dt("wO", [128, KC * 1024], F32, kind="ExternalInput").ap()
    pc_d = dt("pc", [128, NPC], F32, kind="ExternalInput").ap()
    mats_d = dt("mats", [128, 1280], F32, kind="ExternalInput").ap()
    rows_d = dt("rows", [2, 1024], F32, kind="ExternalInput").ap()
    bc_d = dt("bc", [128, 3072], F32, kind="ExternalInput").ap()
    out_d = dt("out", [NTOK, D], F32, kind="ExternalOutput").ap()
    mscr_d = dt("mscr", [8, 128, NPE * 128], BF16, kind="Internal").ap()

    tk = Tracker()
    with ExitStack() as es:
        def sb(name, shape, dtype):
            return es.enter_context(nc.sbuf_tensor(name, shape, dtype))

        xTb = [sb(f"xT_sb{i}", [128, KC, T], BF16) for i in range(2)]
        cur = {}
        hA = sb("hA", [128, 8, T], BF16)
        big2 = sb("big2", [128, 8, T], BF16)
        sB = sb("sB", [128, 8, T], BF16)
        NRING = 3
        wring = [sb(f"wring{i}", [128, KC, 4, 128], BF16) for i in range(NRING)]
        wres = sb("wres", [128, KC, 1024], BF16)
        Mbuf = [sb(f"Mbuf{i}", [128, NPE, 128], BF16) for i in range(2)]
        negones_bf = sb("negones_bf", [128, 128], BF16)
        hbuf = [sb(f"hbuf{i}", [128, PAD + T], BF16) for i in range(2)]
        tail = sb("tail", [128, 8, PAD], BF16)
        NF = 8
        fs = [sb(f"fs{i}", [128, T], F32) for i in range(NF)]
        NB = 3
        accp = [sb(f"accp{i}", [128, T], F32) for i in range(2)]
        bs = [sb(f"bs{i}", [128, T], BF16) for i in range(NB)]
        pc = sb("pc_sb", [128, NPC], F32)
        cmt = sb("cm_sb", [128, 128], F32)
        brow = sb("brow", [2, 2048], BF16)
        blo = sb("blo", [1, 2048], BF16)
        bc = sb("bc_sb", [128, 2048], F32)
        Cm = sb("Cm", [128, 8, 128], F32)
        wsbf = sb("wsbf", [128, 8, 128], BF16)
        cbias = sb("cbias", [128, 8], F32)
        ones_bf = sb("ones_bf", [128, 128], BF16)
        onesf = sb("onesf", [128, 128], F32)
        ones_row = sb("ones_row", [2, 128], BF16)
        stat = sb("stat", [128, 8, 8, 4], F32)
        ps = [es.enter_context(nc.psum_tensor(f"ps{i}", [128, T], F32)) for i in range(4)]

        semnames = ["brs", "pe", "act", "dve", "pool", "xT0", "xT1", "wres", "mb0", "mb1", "ms0", "ms1", "c0", "c1", "c2", "c3", "c4", "c5", "c6", "rw0", "rw1"] + [f"xl{i}" for i in range(8)] + [f"so{i}" for i in range(8)] + [f"wr{i}" for i in range(NRING)]
        for s in semnames:
            tk.sems[s] = es.enter_context(nc.semaphore(s))
        block = es.enter_context(nc.Block())

        cnt = {"ps": 0, "fs": 0, "bs": 0, "st": 0, "ring": 0, "acc": 0}

        def new_ps():
            i = cnt["ps"] % 4
            cnt["ps"] += 1
            return ps[i], ("ps", i)

        def new_fs():
            i = cnt["fs"] % NF
            cnt["fs"] += 1
            return fs[i], ("fs", i)

        def new_bs():
            i = cnt["bs"] % NB
            cnt["bs"] += 1
            return bs[i], ("bs", i)

        def new_stat():
            i = cnt["st"] % 8
            cnt["st"] += 1
            return stat[:, i], ("stat", i)

        FS = lambda i: ("fs", i)
        XR = lambda i: ("fs", 6 + i)
        xr = [fs[6], fs[7]]
        wsT_t = fs[3]
        um_t = fs[4]
        wsm_t = fs[5]
        bsp_t = fs[2]
        tk.op("sp", [], ["pc"], lambda e: e.dma_start(out=pc[:], in_=pc_d), sem="c0", inc=16)
        tk.op("sp", [], ["cm"], lambda e: e.dma_start(out=cmt[:], in_=mats_d[:, 0:128]), sem="c1", inc=16)
        tk.op("sp", [], [FS(4)], lambda e: e.dma_start(out=um_t[:, 0:128], in_=mats_d[:, 128:256]), sem="c2", inc=16)
        tk.op("sp", [], [FS(3)], lambda e: e.dma_start(out=wsT_t[:], in_=mats_d[:, 256:1280]), sem="c3", inc=16)
        tk.op("sp", [], [XR(0)], lambda e: e.dma_start(out=xr[0][0:1, :], in_=rows_d[0:1, :]), sem="rw0", inc=16)
        tk.op("sp", [], [XR(1)], lambda e: e.dma_start(out=xr[1][0:1, :], in_=rows_d[1:2, :]), sem="rw1", inc=16)
        tk.op("sp", [], [FS(2)], lambda e: e.dma_start(out=bsp_t[:], in_=bc_d[:, 0:1024]), sem="c5", inc=16)
        tk.op("sp", [], ["bc"], lambda e: e.dma_start(out=bc[:], in_=bc_d[:, 1024:3072]), sem="c6", inc=16)
        tk.op("pool", [], ["ones_bf"], lambda e: e.memset(ones_bf[:], 1.0 / 128))
        tk.op("pool", [], ["onesf"], lambda e: e.memset(onesf[:], 1.0))
        tk.op("pool", [], ["negones_bf"], lambda e: e.memset(negones_bf[:], -1.0 / 128))
        tk.op("pool", [], ["ones_row"], lambda e: e.memset(ones_row[:], 1.0))
        tk.op("pool", [], [("tail", c) for c in range(8)], lambda e: e.memset(tail[:], 0.0))
        cm_ap = cmt[:]
        p0, p0id = new_ps()
        tk.op("pe", ["pc", "onesf"], [p0id], lambda e: e.matmul(p0[:, 0:8], onesf[:], pc[:, PC_CB:PC_CB + 8], start=True, stop=True))
        tk.op("dve", [p0id, "pc"], ["cbias"], lambda e: e.scalar_tensor_tensor(
            out=cbias[:], in0=p0[:, 0:8], scalar=-1.0 / 128, in1=pc[:, PC_CB:PC_CB + 8], op0=ALU.mult, op1=ALU.add))
        tk.op("dve", [FS(3), FS(4)], [FS(5)], lambda e: e.tensor_tensor(
            out=wsm_t[:].rearrange("p (h t) -> p h t", h=8), in0=wsT_t[:].rearrange("p (h t) -> p h t", h=8),
            in1=um_t[:, 0:128].unsqueeze(1).broadcast_to([128, 8, 128]), op=ALU.mult))
        tk.op("dve", [FS(5)], ["wsbf"], lambda e: e.tensor_copy(out=wsbf[:].rearrange("p h t -> p (h t)"), in_=wsm_t[:]))
        p1, p1id = new_ps()
        tk.op("pe", [FS(5), "onesf"], [p1id], [
            (lambda e, j=j: e.matmul(p1[:, j * 512:(j + 1) * 512], onesf[:], wsm_t[:, j * 512:(j + 1) * 512], start=True, stop=True))
            for j in range(2)])
        for h in range(8):
            tk.op("dve", [p1id, "pc", FS(2)], [("Cm", h)], lambda e, h=h: e.scalar_tensor_tensor(
                out=Cm[:, h, :], in0=p1[:, h * 128:(h + 1) * 128], scalar=pc[:, PC_LVB + h:PC_LVB + h + 1],
                in1=bsp_t[:, h * 128:(h + 1) * 128], op0=ALU.mult, op1=ALU.add))
        for r in range(2):
            src = xr[r][0:1, :]
            tmp = fs[r][0:1, :]
            tk.op("dve", [XR(r)], ["brow_hi%d" % r], lambda e, r=r, src=src: e.tensor_copy(out=brow[0:1, r * 1024:(r + 1) * 1024], in_=src))
            tk.op("dve", ["brow_hi%d" % r], [FS(r)], lambda e, r=r, tmp=tmp: e.tensor_copy(out=tmp, in_=brow[0:1, r * 1024:(r + 1) * 1024]))
            tk.op("dve", [FS(r), XR(r)], [FS(r)], lambda e, src=src, tmp=tmp: e.tensor_tensor(out=tmp, in0=src, in1=tmp, op=ALU.subtract))
            tk.op("dve", [FS(r)], ["blo%d" % r], lambda e, r=r, tmp=tmp: e.tensor_copy(out=blo[:, r * 1024:(r + 1) * 1024], in_=tmp))
        tk.op("sp", ["blo0", "blo1"], ["brow_lo"], lambda e: e.dma_start(out=brow[1:2, :], in_=blo[:]), sem="brs", inc=16)
        BROW = ["brow_hi0", "brow_hi1", "brow_lo"]
        for c in range(8):
            mb_ = Mbuf[c % 2]
            tk.op("dve", ["cm", "pc"], [("Mbuf", c % 2)], lambda e, c=c, mb_=mb_: e.tensor_tensor(
                out=mb_[:], in0=cm_ap.unsqueeze(1).broadcast_to([128, NPE, 128]),
                in1=pc[:, PC_CW + c * CONV_K + ND:PC_CW + (c + 1) * CONV_K].unsqueeze(2).broadcast_to([128, NPE, 128]), op=ALU.mult))
            tk.op("sp", [("Mbuf", c % 2)], [("mscr", c)], lambda e, c=c, mb_=mb_: e.dma_start(
                out=mscr_d[c], in_=mb_[:].rearrange("p k c -> p (k c)")), sem=f"ms{c % 2}", inc=16)

        units = []
        for sbi in range(NSB):
            units.append(("A", sbi, 0))
            units.append(("A", sbi, 1))
            for c in range(2, 8):
                units.append(("A", sbi, c))
                units.append(("G", sbi, c - 2))
            units.append(("G", sbi, 6))
            units.append(("G", sbi, 7))
            for h in range(8):
                units.append(("H", sbi, h))
            for j in range(8):
                units.append(("M", sbi, j))
        unit_slot = {}
        loaded = set()

        def load_unit(n):
            if n >= len(units) or n in loaded:
                return
            loaded.add(n)
            kind, sbi, i = units[n]
            slot = cnt["ring"] % NRING
            cnt["ring"] += 1
            unit_slot[(kind, sbi, i)] = slot
            ns = {"A": 2, "G": 1, "H": 2, "M": 4}[kind]
            src = {"A": wA_d, "G": wG_d, "H": wB_d, "M": wM_d}[kind][i]
            dst = wring[slot][:, :, 0:ns, :]
            tk.op("pool", [], [("wring", slot)], lambda e: e.dma_start(
                out=dst, in_=src.rearrange("p (k s c) -> p k s c", k=KC, s=ns)), sem=f"wr{slot}", inc=16)

        unit_index = {u: n for n, u in enumerate(units)}

        def proj_job(wslot, s, wid, psid_t, extra_reads=()):
            pst, psid = psid_t
            xT = cur["xT"]
            fns = []
            for k in range(KC):
                for blk in range(2):
                    fns.append(lambda e, k=k, blk=blk: e.matmul(
                        pst[:, blk * 512:(blk + 1) * 512], wring[wslot][:, k, s, :], xT[:, k, blk * 512:(blk + 1) * 512],
                        start=(k == 0), stop=(k == KC - 1)))
            tk.op("pe", [wid, cur["xTid"]] + list(extra_reads), [psid], fns)

        def contract_job(wslot, s, wid, src, srcids, psid_t):
            pst, psid = psid_t
            fns = []
            for k in range(KC):
                for blk in range(2):
                    fns.append(lambda e, k=k, blk=blk: e.matmul(
                        pst[:, blk * 512:(blk + 1) * 512], wring[wslot][:, k, s, :], src[:, k, blk * 512:(blk + 1) * 512],
                        start=(k == 0), stop=(k == KC - 1)))
            tk.op("pe", [wid] + list(srcids), [psid], fns)

        def small_rstd(st, stid, n, n_inv):
            tk.op("dve", [stid], [stid], lambda e: e.tensor_scalar(out=st[:, 2:4, 0:n], in0=st[:, 0:2, 0:n], scalar1=n_inv, scalar2=None, op0=ALU.mult))
            tk.op("dve", [stid], [stid], lambda e: e.tensor_tensor(out=st[:, 4, 0:n], in0=st[:, 2, 0:n], in1=st[:, 2, 0:n], op=ALU.mult))
            tk.op("dve", [stid], [stid], lambda e: e.tensor_tensor(out=st[:, 4, 0:n], in0=st[:, 3, 0:n], in1=st[:, 4, 0:n], op=ALU.subtract))
            tk.op("act", [stid], [stid], lambda e: e.activation(out=st[:, 5, 0:n], in_=st[:, 4, 0:n], func=AF.Ln, bias=EPS, scale=1.0))
            tk.op("act", [stid], [stid], lambda e: e.activation(out=st[:, 6, 0:n], in_=st[:, 5, 0:n], func=AF.Exp, scale=-0.5))
            tk.op("dve", [stid], [stid], lambda e: e.scalar_tensor_tensor(
                out=st[:, 7, 0:n], in0=st[:, 2, 0:n], scalar=-1.0, in1=st[:, 6, 0:n], op0=ALU.mult, op1=ALU.mult))

        def pcc(col):
            return pc[:, col:col + 1]

        for sbi in range(NSB):
            half = sbi % 2
            tok0 = sbi * T
            def load_xT(sb_):
                if sb_ >= NSB:
                    return
                xt_ = xTb[sb_ % 2]
                for hh in range(2):
                    tk.op("pool", [], [("xT", sb_ % 2)], lambda e, hh=hh, sb_=sb_, xt_=xt_: e.dma_start(
                        out=xt_[:, hh * 4:(hh + 1) * 4, :],
                        in_=xT_d[sb_].rearrange("p (k t) -> p k t", k=KC)[:, hh * 4:(hh + 1) * 4, :]), sem=f"xT{sb_ % 2}", inc=16)
            if sbi == 0:
                load_xT(0)
            cur["xT"] = xTb[sbi % 2]
            cur["xTid"] = ("xT", sbi % 2)
            xT = cur["xT"]
            xTid = cur["xTid"]
            uA0 = unit_index[("A", sbi, 0)]
            load_unit(uA0)
            load_unit(uA0 + 1)

            A = {}

            def a_front(c):
                load_unit(unit_index[("A", sbi, c)] + 2)
                slot = unit_slot[("A", sbi, c)]
                wid = ("wring", slot)
                hb = hbuf[c % 2]
                hbid = ("hbuf", c % 2)
                mb_ = Mbuf[c % 2]
                mbid = ("Mbuf", c % 2)
                tk.op("sp", [("mscr", c)], [mbid], lambda e: e.dma_start(
                    out=mb_[:].rearrange("p k c -> p (k c)"), in_=mscr_d[c]), sem=f"mb{c % 2}", inc=16)
                if half == 0:
                    tk.op("dve", [], [hbid], lambda e: e.memset(hb[:, 0:PAD], 0.0))
                else:
                    tk.op("dve", [("tail", c)], [hbid], lambda e: e.tensor_copy(out=hb[:, 0:PAD], in_=tail[:, c, :]))
                pg = new_ps()
                proj_job(slot, 0, wid, pg)
                sig, sigid = new_fs()
                tk.op("act", [pg[1], "pc"], [sigid], lambda e: e.activation(
                    out=sig[:], in_=pg[0][:], func=AF.Sigmoid, bias=pcc(PC_BIN + CH_GATE + c), scale=1.0))
                pv = new_ps()
                proj_job(slot, 1, wid, pv)
                tk.op("dve", [pv[1], sigid, "pc"], [hbid], lambda e: e.scalar_tensor_tensor(
                    out=hb[:, PAD:PAD + T], in0=pv[0][:], scalar=pcc(PC_BIN + CH_VAL + c), in1=sig[:], op0=ALU.add, op1=ALU.mult))
                if half == 0:
                    tk.op("dve", [hbid], [("tail", c)], lambda e: e.tensor_copy(out=tail[:, c, :], in_=hb[:, T:T + PAD]), force_sync=True)
                A[c] = {"slot": slot, "wid": wid}

            def a_taps(c):
                hb = hbuf[c % 2]
                hbid = ("hbuf", c % 2)
                ai = cnt["acc"] % 2
                cnt["acc"] += 1
                acc, accid = accp[ai], ("accp", ai)
                w0 = PC_CW + c * CONV_K
                tk.op("dve", [hbid, "pc"], [accid], lambda e: e.tensor_scalar(
                    out=acc[:], in0=hb[:, 0:T], scalar1=pcc(w0), scalar2=None, op0=ALU.mult))
                for k in range(1, ND):
                    tk.op("dve", [hbid, "pc", accid], [accid], lambda e, k=k: e.scalar_tensor_tensor(
                        out=acc[:], in0=hb[:, k:k + T], scalar=pcc(w0 + k), in1=acc[:], op0=ALU.mult, op1=ALU.add))
                A[c].update(acc=acc, accid=accid)

            def a_cast(c):
                a = A[c]
                accb, accbid = new_bs()
                tk.op("act", [a["accid"]], [accbid], lambda e: e.activation(out=accb[:], in_=a["acc"][:], func=AF.Identity))
                a.update(accb=accb, accbid=accbid)

            def a_convA(c):
                a = A[c]
                hb = hbuf[c % 2]
                hbid = ("hbuf", c % 2)
                mb_ = Mbuf[c % 2]
                mbid = ("Mbuf", c % 2)
                pcv = new_ps()
                fns = []
                for k in range(ND, CONV_K):
                    for blk in range(2):
                        fns.append(lambda e, k=k, blk=blk: e.matmul(
                            pcv[0][:, blk * 512:(blk + 1) * 512], mb_[:, k - ND, :], hb[:, blk * 512 + k: blk * 512 + k + 512],
                            start=(k == ND), stop=False))
                tk.op("pe", [hbid, mbid], [pcv[1]], fns)
                a.update(pcv=pcv)

            def a_convB(c):
                a = A[c]
                pcv = a["pcv"]
                fns = []
                for blk in range(2):
                    fns.append(lambda e, blk=blk: e.matmul(
                        pcv[0][:, blk * 512:(blk + 1) * 512], negones_bf[:], a["accb"][:, blk * 512:(blk + 1) * 512],
                        start=False, stop=True))
                tk.op("pe", [a["accbid"], "negones_bf"], [pcv[1]], fns)
                cen, cenid = a["acc"], a["accid"]
                tk.op("dve", [pcv[1], "cbias", cenid], [cenid], lambda e: e.scalar_tensor_tensor(
                    out=cen[:], in0=pcv[0][:], scalar=cbias[:, c:c + 1], in1=cen[:], op0=ALU.add, op1=ALU.add))
                sq, sqid = new_bs()
                tk.op("act", [cenid], [sqid], lambda e: e.activation(out=sq[:], in_=cen[:], func=AF.Square))
                a.update(cen=cen, cenid=cenid, sq=sq, sqid=sqid)

            def a_back1(c):
                a = A[c]
                pvar = new_ps()
                tk.op("pe", [a["sqid"], "ones_bf"], [pvar[1]], [
                    (lambda e, blk=blk: e.matmul(pvar[0][:, blk * 512:(blk + 1) * 512], ones_bf[:], a["sq"][:, blk * 512:(blk + 1) * 512], start=True, stop=True))
                    for blk in range(2)])
                std, stdid = new_fs()
                tk.op("act", [pvar[1]], [stdid], lambda e: e.activation(out=std[:], in_=pvar[0][:], func=AF.Ln, bias=EPS, scale=1.0))
                tk.op("act", [stdid], [stdid], lambda e: e.activation(out=std[:], in_=std[:], func=AF.Exp, scale=-0.5))
                cen, cenid = a["cen"], a["cenid"]
                tk.op("dve", [stdid, cenid], [cenid], lambda e: e.tensor_tensor(out=cen[:], in0=cen[:], in1=std[:], op=ALU.mult))
                a.update(std=std, stdid=stdid)

            def a_back2(c):
                a = A[c]
                std, stdid, cen, cenid = a["std"], a["stdid"], a["cen"], a["cenid"]
                load_unit(unit_index[("G", sbi, c)] + 2)
                gslot = unit_slot[("G", sbi, c)]
                pag = new_ps()
                proj_job(gslot, 0, ("wring", gslot), pag)
                s2, s2id = new_fs()
                tk.op("act", [pag[1], "pc"], [s2id], lambda e: e.activation(
                    out=s2[:], in_=pag[0][:], func=AF.Silu, bias=pcc(PC_BIN + CH_AG + c), scale=1.0))
                tk.op("act", [cenid, "pc"], [stdid], lambda e: e.activation(
                    out=std[:], in_=cen[:], func=AF.Silu, bias=pcc(PC_GNB + c), scale=pcc(PC_GNG + c)))
                tk.op("dve", [stdid, s2id], [("hA", c)], lambda e: e.tensor_tensor(out=hA[:, c, :], in0=std[:], in1=s2[:], op=ALU.mult))

            V = {}

            VG = 2

            def v_front(g, xT_l=xT):
                st, stid = new_stat()
                V[g] = dict(st=st, stid=stid, vg=[])
                for i in range(VG):
                    tt = g * VG + i
                    pv = new_ps()
                    fns = []
                    for k in range(KC):
                        for hf in range(2):
                            fns.append(lambda e, k=k, hf=hf, tt=tt, pv=pv: e.matmul(
                                pv[0][:, hf * 512:(hf + 1) * 512], xT_l[:, k, tt * 128:(tt + 1) * 128], wres[:, k, hf * 512:(hf + 1) * 512],
                                start=(k == 0), stop=False))
                    for hf in range(2):
                        fns.append(lambda e, hf=hf, pv=pv: e.matmul(
                            pv[0][:, hf * 512:(hf + 1) * 512], ones_row[:], brow[:, hf * 512:(hf + 1) * 512], start=False, stop=True))
                    tk.op("pe", [cur["xTid"], "wres", "ones_row"] + BROW, [pv[1]], fns)
                    vg, vgid = new_fs()
                    tk.op("act", [pv[1]], [vgid, stid], lambda e, pv=pv, vg=vg, i=i: e.activation(
                        out=vg[:], in_=pv[0][:], func=AF.Gelu_apprx_tanh, accum_out=st[:, 0, i:i + 1]))
                    junk, junkid = new_bs()
                    tk.op("act", [vgid, stid], [junkid, stid], lambda e, vg=vg, junk=junk, i=i: e.activation(
                        out=junk[:], in_=vg[:], func=AF.Square, accum_out=st[:, 1, i:i + 1]))
                    V[g]["vg"].append((vg, vgid))

            def v_back(g):
                st, stid = V[g]["st"], V[g]["stid"]
                small_rstd(st, stid, VG, 1.0 / 1024)
                for i in range(VG):
                    tt = g * VG + i
                    vg, vgid = V[g]["vg"][i]
                    tk.op("dve", [vgid, stid], [("big2", tt)], lambda e, vg=vg, i=i, tt=tt: e.tensor_scalar(
                        out=big2[:, tt, :], in0=vg[:], scalar1=st[:, 6, i:i + 1], scalar2=st[:, 7, i:i + 1], op0=ALU.mult, op1=ALU.add))

            Hd = {}

            def h_proj(h):
                load_unit(unit_index[("H", sbi, h)] + 2)
                slot = unit_slot[("H", sbi, h)]
                wid = ("wring", slot)
                pu = new_ps()
                proj_job(slot, 0, wid, pu)
                ug, ugid = new_fs()
                tk.op("act", [pu[1], "pc"], [ugid], lambda e: e.activation(
                    out=ug[:], in_=pu[0][:], func=AF.Gelu_apprx_tanh, bias=pcc(PC_BIN + CH_U + h), scale=1.0))
                pb = new_ps()
                proj_job(slot, 1, wid, pb)
                sg, sgid = new_fs()
                tk.op("act", [pb[1], "pc"], [sgid], lambda e: e.activation(
                    out=sg[:], in_=pb[0][:], func=AF.Silu, bias=pcc(PC_BIN + CH_BG + h), scale=1.0))
                Hd[h] = dict(ug=ug, ugid=ugid, sg=sg, sgid=sgid)

            def h_spatial(h):
                d = Hd[h]
                psp = new_ps()
                tk.op("pe", [("big2", tt) for tt in range(8)] + ["wsbf"], [psp[1]], [
                    (lambda e, tt=tt: e.matmul(psp[0][:, tt * 128:(tt + 1) * 128], big2[:, tt, h * 128:(h + 1) * 128], wsbf[:, h, :], start=True, stop=True))
                    for tt in range(8)])
                vm, vmid = new_fs()
                tk.op("dve", [psp[1], "pc", ("Cm", h)], [vmid], lambda e: e.scalar_tensor_tensor(
                    out=vm[:].rearrange("p (n t) -> p n t", t=128), in0=psp[0][:].rearrange("p (n t) -> p n t", t=128),
                    scalar=pcc(PC_LVG + h), in1=Cm[:, h, :].unsqueeze(1).broadcast_to([128, 8, 128]), op0=ALU.mult, op1=ALU.add))
                tk.op("dve", [vmid, d["ugid"]], [vmid], lambda e: e.tensor_tensor(out=vm[:], in0=vm[:], in1=d["ug"][:], op=ALU.mult))
                tk.op("dve", [vmid, d["sgid"]], [("sB", h)], lambda e: e.tensor_tensor(out=sB[:, h, :], in0=vm[:], in1=d["sg"][:], op=ALU.mult))

            def m_unit(j):
                load_unit(unit_index[("M", sbi, j)] + 2)
                slot = unit_slot[("M", sbi, j)]
                wid = ("wring", slot)
                pga = new_ps()
                proj_job(slot, 0, wid, pga)
                ga, gaid = new_fs()
                tk.op("act", [pga[1], "pc"], [gaid], lambda e: e.activation(
                    out=ga[:], in_=pga[0][:], func=AF.Sigmoid, bias=pcc(PC_BIN + CH_MA + j), scale=1.0))
                pya = new_ps()
                contract_job(slot, 2, wid, hA, [("hA", c) for c in range(8)], pya)
                tk.op("dve", [pya[1], gaid], [gaid], lambda e: e.tensor_tensor(out=ga[:], in0=pya[0][:], in1=ga[:], op=ALU.mult))
                pgb = new_ps()
                proj_job(slot, 1, wid, pgb)
                gb, gbid = new_fs()
                tk.op("act", [pgb[1], "pc"], [gbid], lambda e: e.activation(
                    out=gb[:], in_=pgb[0][:], func=AF.Sigmoid, bias=pcc(PC_BIN + CH_MB + j), scale=1.0))
                pyb = new_ps()
                contract_job(slot, 3, wid, sB, [("sB", h) for h in range(8)], pyb)
                tk.op("dve", [pyb[1], gbid], [gbid], lambda e: e.tensor_tensor(out=gb[:], in0=pyb[0][:], in1=gb[:], op=ALU.mult))
                tk.op("dve", [gaid, gbid], [("big2", j)], lambda e: e.tensor_tensor(out=big2[:, j, :], in0=ga[:], in1=gb[:], op=ALU.add))

            FG = 2
            X = {}
            F = {}

            def x_load(tt, tok0=tok0):
                if tt >= 8 or tt in X:
                    return
                xt_, xid = new_fs()
                r0 = tok0 + tt * 128
                tk.op("sp", [], [xid], lambda e: e.dma_start(out=xt_[:], in_=xtok_d[r0:r0 + 128, :]), sem=f"xl{tt}", inc=16)
                X[tt] = (xt_, xid)

            def f_front(g):
                st, stid = new_stat()
                F[g] = dict(st=st, stid=stid)
                for i in range(FG):
                    tt = g * FG + i
                    x_load(tt)
                    xrt, xid = X[tt]
                    pf = new_ps()
                    fns = []
                    for k in range(KC):
                        for hf in range(2):
                            fns.append(lambda e, k=k, hf=hf, tt=tt, pf=pf: e.matmul(
                                pf[0][:, hf * 512:(hf + 1) * 512], big2[:, k, tt * 128:(tt + 1) * 128], wres[:, k, hf * 512:(hf + 1) * 512],
                                start=(k == 0), stop=False))
                    for hf in range(2):
                        fns.append(lambda e, hf=hf, pf=pf: e.matmul(
                            pf[0][:, hf * 512:(hf + 1) * 512], ones_row[:], brow[:, 1024 + hf * 512:1024 + (hf + 1) * 512], start=False, stop=True))
                    tk.op("pe", [("big2", j) for j in range(8)] + ["wres", "ones_row"] + BROW, [pf[1]], fns)
                    tk.op("dve", [pf[1], xid], [xid], lambda e, xrt=xrt, pf=pf: e.scalar_tensor_tensor(
                        out=xrt[:], in0=xrt[:], scalar=ALPHA, in1=pf[0][:], op0=ALU.mult, op1=ALU.add))
                    j1, j1id = new_bs()
                    tk.op("act", [xid], [j1id, stid], lambda e, xrt=xrt, j1=j1, i=i: e.activation(
                        out=j1[:], in_=xrt[:], func=AF.Identity, accum_out=st[:, 0, i:i + 1]))
                    j2, j2id = new_bs()
                    tk.op("act", [xid, stid], [j2id, stid], lambda e, xrt=xrt, j2=j2, i=i: e.activation(
                        out=j2[:], in_=xrt[:], func=AF.Square, accum_out=st[:, 1, i:i + 1]))

            def f_back(g, tok0=tok0):
                st, stid = F[g]["st"], F[g]["stid"]
                small_rstd(st, stid, FG, 1.0 / 1024)
                for i in range(FG):
                    tt = g * FG + i
                    xrt, xid = X[tt]
                    r0 = tok0 + tt * 128
                    tk.op("act", [xid, stid], [xid], lambda e, xrt=xrt, i=i: e.activation(
                        out=xrt[:], in_=xrt[:], func=AF.Identity, bias=st[:, 7, i:i + 1], scale=st[:, 6, i:i + 1]))
                    tk.op("dve", [xid, "bc"], [xid], lambda e, xrt=xrt: e.tensor_tensor(out=xrt[:], in0=xrt[:], in1=bc[:, 0:1024], op=ALU.mult))
                    tk.op("dve", [xid, "bc"], [xid], lambda e, xrt=xrt: e.tensor_tensor(out=xrt[:], in0=xrt[:], in1=bc[:, 1024:2048], op=ALU.add))
                    tk.op("sp", [xid], [("out", tt, sbi)], lambda e, xrt=xrt, r0=r0: e.dma_start(out=out_d[r0:r0 + 128, :], in_=xrt[:]), sem=f"so{tt}", inc=16)

            for it in range(10):
                if 1 <= it <= 8:
                    a_taps(it - 1)
                if it < 8:
                    a_front(it)
                if it == 5:
                    load_xT(sbi + 1)
                if it == 3:
                    tk.op("pool", [], ["wres"], lambda e: e.dma_start(out=wres[:], in_=wV_d.rearrange("p (k c) -> p k c", k=KC)), sem="wres", inc=16)
                if it == 8:
                    v_front(0)
                if 1 <= it <= 8:
                    a_convA(it - 1)
                if it >= 2:
                    a_back1(it - 2)
                if 1 <= it <= 8:
                    a_cast(it - 1)
                if it >= 2:
                    a_back2(it - 2)
                if 1 <= it <= 8:
                    a_convB(it - 1)
                if it == 8:
                    v_front(1)
                    v_back(0)
            v_front(2)
            v_back(1)
            v_front(3)
            v_back(2)
            uH0 = unit_index[("H", sbi, 0)]
            load_unit(uH0)
            load_unit(uH0 + 1)
            h_proj(0)
            v_back(3)
            h_proj(1)
            tk.op("pool", [], ["wres"], lambda e: e.dma_start(out=wres[:], in_=wO_d.rearrange("p (k c) -> p k c", k=KC)), sem="wres", inc=16)
            h_spatial(0)
            for h in range(2, 8):
                h_proj(h)
                h_spatial(h - 1)
            h_spatial(7)
            for j in range(8):
                m_unit(j)
            NG = 8 // FG
            for tt in range(2 * FG):
                x_load(tt)
            f_front(0)
            for g in range(NG):
                if g + 1 < NG:
                    f_front(g + 1)
                for tt in range((g + 2) * FG, (g + 3) * FG):
                    x_load(tt)
                f_back(g)

        def fin(e):
            ins = None
            for s in [f"so{i}" for i in range(8)]:
                e.wait_ge(tk.sems[s], tk.count[s])

        @block.sync
        def _(e):
            tk.replay("sp", e)
            fin(e)

        @block.tensor
        def _(e):
            tk.replay("pe", e)

        @block.scalar
        def _(e):
            tk.replay("act", e)

        @block.vector
        def _(e):
            tk.replay("dve", e)

        @block.gpsimd
        def _(e):
            tk.replay("pool", e)
    return nc


def _prep_weights(inp):
    f = np.float32
    w_in = np.asarray(inp["w_in"], f)
    w_pa = np.asarray(inp["w_pa"], f)
    w_pb = np.asarray(inp["w_pb"], f)
    w_o = np.asarray(inp["w_o"], f)

    def kp(w):
        return w.reshape(KC, 128, -1).transpose(1, 0, 2)

    def cols(w, off, i):
        return kp(w[:, off + i * 128: off + (i + 1) * 128])

    wA = np.stack([np.stack([cols(w_in, 1024, c), cols(w_in, 0, c)], axis=2) for c in range(8)])
    wG = np.stack([cols(w_in, 2048, c) for c in range(8)])
    wB = np.stack([np.stack([cols(w_in, 3072, h), cols(w_in, 5120, h)], axis=2) for h in range(8)])
    wM = np.stack([np.stack([cols(w_in, 6144, j), cols(w_in, 7168, j), cols(w_pa, 0, j), cols(w_pb, 0, j)], axis=2) for j in range(8)])
    wV = kp(w_in[:, 4096:5120])
    wO = kp(w_o)
    pc = np.zeros((128, NPC), f)
    pc[:, PC_BIN:PC_BIN + 64] = np.asarray(inp["b_in"], f).reshape(64, 128).T
    for off, name in ((PC_CB, "conv_b"), (PC_GNG, "gn_g"), (PC_GNB, "gn_b"), (PC_LVG, "ln_v_g"), (PC_LVB, "ln_v_b")):
        pc[:, off:off + 8] = np.asarray(inp[name], f).reshape(8, 128).T
    cw = np.asarray(inp["conv_w"], f)
    pc[:, PC_CW:] = cw.reshape(CONV_K, 8, 128).transpose(2, 1, 0).reshape(128, 8 * CONV_K)
    mats = np.zeros((128, 1280), f)
    mats[:, 0:128] = np.eye(128, dtype=f) - f(1.0 / 128)
    mats[:, 128:256] = np.triu(np.ones((128, 128), f))
    mats[:, 256:] = np.asarray(inp["w_spatial"], f).transpose(2, 0, 1).reshape(128, 1024)
    rows = np.stack([np.asarray(inp["b_in"], f)[4096:5120], np.asarray(inp["b_o"], f)])
    bc = np.concatenate([np.asarray(inp["b_spatial"], f).reshape(-1), np.asarray(inp["ln_out_g"], f), np.asarray(inp["ln_out_b"], f)])
    bc = np.ascontiguousarray(np.broadcast_to(bc[None, :], (128, 3072)))
    c = np.ascontiguousarray
    return dict(wA=c(wA.reshape(8, 128, -1)), wG=c(wG.reshape(8, 128, -1)), wB=c(wB.reshape(8, 128, -1)), wM=c(wM.reshape(8, 128, -1)),
                wV=c(wV.reshape(128, -1)), wO=c(wO.reshape(128, -1)), pc=pc, mats=mats, rows=c(rows), bc=bc)


_NC_CACHE = {}


def kernel(**inputs):
    x = np.asarray(inputs["x"], np.float32)
    shared = _prep_weights(inputs)
    n = 8
    in_maps = []
    for ci in range(n):
        xt = np.ascontiguousarray(x[2 * ci:2 * ci + 2].reshape(NTOK, D))
        xT = np.ascontiguousarray(xt.reshape(NSB, T, KC, 128).transpose(0, 3, 2, 1)).reshape(NSB, 128, KC * T)
        m = dict(shared)
        m["xT"] = xT
        m["xtok"] = xt
        in_maps.append(m)
    if "nc" not in _NC_CACHE:
        _NC_CACHE["nc"] = build_nc()
    nc = _NC_CACHE["nc"]
    res = run_bass_kernel_spmd(nc, in_maps, core_ids=list(range(n)))
    out = np.stack([np.asarray(r["out"], np.float32).reshape(2, 2048, D) for r in res.results]).reshape(16, 2048, D)
    return out
```
